# Optimizing a Trainium2 kernel written in Bass

```python
import jax, jax.numpy as jnp
from jax import lax
import numpy as np

D_MODEL = 2048
BATCH = 2
SEQ = 4096
DEPTH = 1

CHUNK = 64
SGU_BLOCK = 128
D_SGU = D_MODEL // 2
SGU_HEADS = 8
SGU_HEAD_DIM = D_SGU // SGU_HEADS
D_POOL = D_MODEL // 2
POOL_WINDOWS = (2, 4, 8, 16)
POOL_GROUPS = len(POOL_WINDOWS)
POOL_GROUP_DIM = D_POOL // POOL_GROUPS
D_FF = -(-8 * D_MODEL // (3 * 256)) * 256
D_IN = 2 * D_SGU + D_POOL + 2 * D_MODEL
EPS = 1e-6

kernel_name = "hybrid_sgu_pool_gated_block"


def rms_norm(x, g):
    xf = x.astype(jnp.float32)
    y = xf * lax.rsqrt(jnp.mean(xf * xf, axis=-1, keepdims=True) + EPS)
    return (y * g.astype(jnp.float32)).astype(x.dtype)


def layer_norm(x, g, b):
    xf = x.astype(jnp.float32)
    mu = jnp.mean(xf, axis=-1, keepdims=True)
    xc = xf - mu
    y = xc * lax.rsqrt(jnp.mean(xc * xc, axis=-1, keepdims=True) + EPS)
    return (y * g.astype(jnp.float32) + b.astype(jnp.float32)).astype(x.dtype)


def sgu_mixer(u, v, ln_g, ln_b, w_s, b_s):
    bsz, s, _ = v.shape
    nb = s // SGU_BLOCK
    v = layer_norm(v, ln_g, ln_b)
    idx = jnp.arange(SGU_BLOCK)
    mask = (idx[:, None] // CHUNK) >= (idx[None, :] // CHUNK)
    w = jnp.where(mask[None], w_s, 0)
    vb = v.reshape(bsz, nb, SGU_BLOCK, SGU_HEADS, SGU_HEAD_DIM)
    mixed = jnp.einsum('hij,bnjhd->bnihd', w, vb) + b_s.T[None, None, :, :, None]
    return u * mixed.reshape(bsz, s, D_SGU)


def pool_mixer(p, w_pool, scale):
    bsz, s, _ = p.shape
    pf = p.astype(jnp.float32)
    c = jnp.cumsum(pf, axis=1)
    count = jnp.arange(1, s + 1, dtype=jnp.float32)
    means = []
    for gi, w in enumerate(POOL_WINDOWS):
        cg = c[..., gi * POOL_GROUP_DIM:(gi + 1) * POOL_GROUP_DIM]
        lag = jnp.pad(cg[:, :s - w], ((0, 0), (w, 0), (0, 0)))
        means.append((cg - lag) / jnp.minimum(count, w)[None, :, None])
    pooled = (jnp.concatenate(means, axis=-1) - pf).astype(p.dtype)
    pooled = pooled.reshape(bsz, s, POOL_GROUPS, POOL_GROUP_DIM)
    y = jnp.einsum('bsgc,gcd->bsgd', pooled, w_pool).reshape(bsz, s, D_POOL)
    return y * scale


def setup_inputs(seed: int = 0) -> dict:
    key = jax.random.key(seed)
    ks = jax.random.split(key, 20)
    f32 = jnp.float32

    def nrm(k, shape, fan_in):
        return jax.random.normal(k, shape, f32) * (fan_in ** -0.5)

    def gain(k, shape):
        return 1.0 + 0.02 * jax.random.normal(k, shape, f32)

    L = DEPTH
    return {
        "x": jax.random.normal(ks[0], (BATCH, SEQ, D_MODEL), f32),
        "norm1_pre": gain(ks[1], (L, D_MODEL)),
        "w_in": nrm(ks[2], (L, D_MODEL, D_IN), D_MODEL),
        "v_ln_g": gain(ks[3], (L, D_SGU)),
        "v_ln_b": 0.02 * jax.random.normal(ks[4], (L, D_SGU), f32),
        "sgu_w": nrm(ks[5], (L, SGU_HEADS, SGU_BLOCK, SGU_BLOCK), SGU_BLOCK),
        "sgu_b": gain(ks[6], (L, SGU_HEADS, SGU_BLOCK)),
        "pool_w": nrm(ks[7], (L, POOL_GROUPS, POOL_GROUP_DIM, POOL_GROUP_DIM), POOL_GROUP_DIM),
        "pool_scale": gain(ks[8], (L, D_POOL)),
        "w_a_out": nrm(ks[9], (L, D_SGU, D_MODEL), D_SGU),
        "w_b_out": nrm(ks[10], (L, D_POOL, D_MODEL), D_POOL),
        "w_mix_out": nrm(ks[11], (L, D_MODEL, D_MODEL), D_MODEL),
        "norm1_post": gain(ks[12], (L, D_MODEL)),
        "norm2_pre": gain(ks[13], (L, D_MODEL)),
        "w_ffn_gate": nrm(ks[14], (L, D_MODEL, D_FF), D_MODEL),
        "w_ffn_up": nrm(ks[15], (L, D_MODEL, D_FF), D_MODEL),
        "w_ffn_down": nrm(ks[16], (L, D_FF, D_MODEL), D_FF),
        "norm2_post": gain(ks[17], (L, D_MODEL)),
    }


def reference(x, norm1_pre, w_in, v_ln_g, v_ln_b, sgu_w, sgu_b, pool_w, pool_scale,
              w_a_out, w_b_out, w_mix_out, norm1_post, norm2_pre, w_ffn_gate,
              w_ffn_up, w_ffn_down, norm2_post):
    s1 = D_SGU
    s2 = 2 * D_SGU
    s3 = s2 + D_POOL
    s4 = s3 + D_MODEL
    for l in range(DEPTH):
        h = rms_norm(x, norm1_pre[l])
        z = jnp.einsum('bsd,de->bse', h, w_in[l])
        uv = jax.nn.gelu(z[..., :s2], approximate=False)
        u, v = uv[..., :s1], uv[..., s1:]
        p = z[..., s2:s3]
        gate_a = jax.nn.sigmoid(z[..., s3:s4])
        gate_b = jax.nn.sigmoid(z[..., s4:])
        y_a = sgu_mixer(u, v, v_ln_g[l], v_ln_b[l], sgu_w[l], sgu_b[l])
        y_b = pool_mixer(p, pool_w[l], pool_scale[l])
        merged = (gate_a * jnp.einsum('bsc,cd->bsd', y_a, w_a_out[l])
                  + gate_b * jnp.einsum('bsc,cd->bsd', y_b, w_b_out[l]))
        mix_out = jnp.einsum('bsd,de->bse', merged, w_mix_out[l])
        x = x + rms_norm(mix_out, norm1_post[l])
        h = rms_norm(x, norm2_pre[l])
        hid = (jax.nn.silu(jnp.einsum('bsd,df->bsf', h, w_ffn_gate[l]))
               * jnp.einsum('bsd,df->bsf', h, w_ffn_up[l]))
        ffn_out = jnp.einsum('bsf,fd->bsd', hid, w_ffn_down[l])
        x = x + rms_norm(ffn_out, norm2_post[l])
    return x
```

```python
import numpy as np
import ml_dtypes
import concourse.bass as bass
import concourse.mybir as mybir
from concourse.bass_utils import run_bass_kernel_spmd

F32 = mybir.dt.float32
BF16 = mybir.dt.bfloat16
U8 = mybir.dt.uint8
AF = mybir.ActivationFunctionType
ALU = mybir.AluOpType
ESZ = {F32: 4, BF16: 2, U8: 1}

D = 2048
DS = 1024
DFF = 5632
NFC = DFF // 128
EPS = 1e-6
NCORES = 8
TOK = 1024
HT = 512
HALO = 16
ARENA_BYTES = 212736
GRAN = 256


class Op:
    __slots__ = ("eng", "fn", "R", "W", "dma", "group", "deps", "signal", "sem", "sigval", "key", "after", "idx")

    def __init__(self, eng, fn, R, W, dma, group):
        self.eng, self.fn, self.R, self.W, self.dma, self.group = eng, fn, R, W, dma, group
        self.deps = []
        self.signal = False
        self.sem = None
        self.sigval = 0
        self.key = None
        self.after = []
        self.idx = -1


class Sched:
    def __init__(self, nc):
        self.nc = nc
        self.ops = []

    def res(self, x):
        if not hasattr(x, "tensor"):
            return [x]
        name = x.tensor.name
        es = ESZ[x.dtype]
        if name == "arena":
            g, sp = GRAN, "sb"
            fine = getattr(self, "fine", None)
        elif name == "psum":
            g, sp = 2048, "ps"
        else:
            return [("dram", name)]
        pstride = x.ap[0][0]
        off = x.offset % pstride if pstride else x.offset
        dims = [(st, cnt) for st, cnt in x.ap[1:] if cnt > 1]
        span = 1
        for st, cnt in dims:
            span += (cnt - 1) * abs(st)
        lo = off * es
        hi = (off + span) * es
        if sp == "sb" and fine and lo >= fine[0] and hi <= fine[1]:
            return [("st", i) for i in range(lo // 4, (hi - 1) // 4 + 1)]
        if dims and dims[-1][0] == 1:
            run = dims[-1][1]
            outer = dims[:-1]
        else:
            run, outer = 1, dims
        nruns = 1
        for st, cnt in outer:
            nruns *= cnt
        if outer and nruns <= 512 and all(st >= 0 for st, _ in outer):
            out = set()
            offs = [0]
            for st, cnt in outer:
                offs = [o + st * i for o in offs for i in range(cnt)]
            for o in offs:
                a = (off + o) * es
                b = (off + o + run) * es
                out.update(range(a // g, (b - 1) // g + 1))
            return [(sp, i) for i in sorted(out)]
        return [(sp, i) for i in range(lo // g, (hi - 1) // g + 1)]

    def add(self, eng, fn, reads=(), writes=(), dma=False, group=None, after=()):
        R, W = [], []
        for x in reads:
            R.extend(self.res(x))
        for x in writes:
            W.extend(self.res(x))
        op = Op(eng, fn, R, W, dma, group)
        if dma:
            op.key = group if group is not None else ("k", eng) + tuple(W[0])
        op.after = list(after)
        op.idx = len(self.ops)
        self.ops.append(op)
        return op

    def analyze(self):
        last_w, readers = {}, {}
        for i, op in enumerate(self.ops):
            deps = {}
            for r in op.R:
                w = last_w.get(r)
                if w is not None:
                    deps.setdefault(w, set()).add("RAW")
                if r[0] == "ps":
                    for rd in readers.get(r, ()):
                        if self.ops[rd].eng != op.eng:
                            deps.setdefault(rd, set()).add("PSRAR")
            for r in op.W:
                w = last_w.get(r)
                if w is not None:
                    deps.setdefault(w, set()).add("WAW")
                for rd in readers.get(r, ()):
                    deps.setdefault(rd, set()).add("WAR")
            for r in op.W:
                last_w[r] = i
                readers[r] = []
            for r in op.R:
                lst = readers.setdefault(r, [])
                if not lst or lst[-1] != i:
                    lst.append(i)
            for a in op.after:
                deps.setdefault(a.idx, set()).add("RAW")
            for d, kinds in deps.items():
                if d == i:
                    continue
                dop = self.ops[d]
                if dop.dma and op.dma and dop.group is not None and dop.group == op.group:
                    need = False
                elif dop.dma or op.dma:
                    need = True
                elif dop.eng == op.eng:
                    need = op.eng != "pe" and bool(kinds - {"PSRAR"})
                else:
                    need = True
                if need:
                    op.deps.append(d)
                    dop.signal = True

    def emit(self, sems_ctx):
        nc = self.nc
        cnt = {}
        group_total = {}
        for op in self.ops:
            if op.dma:
                op.signal = True
                k = ("dma", op.key)
                cnt[k] = cnt.get(k, 0) + 16
                op.sigval = cnt[k]
                op.sem = k
                if op.group is not None:
                    group_total[k] = cnt[k]
            elif op.signal:
                k = ("eng", op.eng)
                cnt[k] = cnt.get(k, 0) + 1
                op.sigval = cnt[k]
                op.sem = k
        for op in self.ops:
            if op.dma and op.group is not None:
                op.sigval = group_total[op.sem]
        semh = {}
        for k in cnt:
            semh[k] = sems_ctx(str(len(semh)))
        self.nsems = len(semh)
        per_eng = {}
        for op in self.ops:
            per_eng.setdefault(op.eng, []).append(op)

        def run(engname, e):
            waited = {}
            for op in per_eng.get(engname, []):
                need = {}
                for d in op.deps:
                    dop = self.ops[d]
                    if need.get(dop.sem, 0) < dop.sigval:
                        need[dop.sem] = dop.sigval
                for s, v in need.items():
                    if waited.get(s, 0) < v:
                        e.wait_ge(semh[s], v)
                        waited[s] = v
                if op.fn is None:
                    continue
                ins = op.fn(e)
                if op.signal:
                    ins.then_inc(semh[op.sem], 16 if op.dma else 1)
        return run


class Arena:
    def __init__(self, ap):
        self.ap = ap
        self.ptr = 0

    def alloc(self, nbytes, dtype, pattern=None, **kw):
        off = (self.ptr + GRAN - 1) // GRAN * GRAN if nbytes >= GRAN else (self.ptr + 31) // 32 * 32
        self.ptr = off + nbytes
        assert self.ptr <= ARENA_BYTES, f"arena overflow {self.ptr}"
        v = self.ap[:, off:off + nbytes].bitcast(dtype)
        if pattern:
            v = v.rearrange(pattern, **kw)
        return v


def build_program(dbg=None, nhalves=2, last_stage=None):
    nc = bass.Bass("TRN2", target_bir_lowering=False)
    dram = {}

    def din(name, shape, dt=F32):
        dram[name] = nc.dram_tensor(name, list(shape), dt, kind="ExternalInput").ap()
        return dram[name]

    x_ext = din("x_ext", [HALO + TOK, D])
    w_v = din("w_v", [2, 128, 16, 512])
    w_u = din("w_u", [8, 128, 16, 128])
    w_p = din("w_p", [8, 128, 16, 128])
    w_gab = din("w_gab", [16, 128, 48, 128])
    w_pool = din("w_pool", [128, 4, 2, 256])
    w_mix = din("w_mix", [4, 128, 16, 512])
    w_gu = din("w_gu", [NFC, 128, 32, 128])
    w_dn = din("w_dn", [4, 11, 128, 4, 512])
    sgwT_d = din("sgwT", [128, 8, 128])
    sgub_d = din("sgub", [128, 8, 128])
    lng_d = din("lng", [128, DS])
    lnb_d = din("lnb", [128, DS])
    gp1_d = din("gp1", [128, D])
    gp2_d = din("gp2", [128, D])
    g1T_d = din("g1T", [128, 16])
    g2T_d = din("g2T", [128, 16])
    pscT_d = din("pscT", [128, 8])
    invf_d = din("invf", [128, 4, 16])
    ident_d = din("ident", [128, 128], BF16)
    out_d = nc.dram_tensor("out", [TOK, D], F32, kind="ExternalOutput").ap()
    dbg_d = {}
    if dbg:
        for name, (shape, dt) in dbg.items():
            dbg_d[name] = nc.dram_tensor("dbg_" + name, list(shape), dt, kind="ExternalOutput").ap()

    sems = []
    import contextlib
    with contextlib.ExitStack() as es:
        arena_t = es.enter_context(nc.sbuf_tensor("arena", [128, ARENA_BYTES], U8))
        psum_t = es.enter_context(nc.psum_tensor("psum", [128, 8, 512], F32))
        A = Arena(arena_t)
        S = Sched(nc)

        def PS(b):
            return psum_t[:, b, :]

        def PSB(b):
            return psum_t[:, b, :].bitcast(BF16)

        bank_ctr = [0]

        def nb():
            b = bank_ctr[0]
            bank_ctr[0] = (b + 1) % 8
            return b

        ident = A.alloc(256, BF16)
        g1T = A.alloc(64, F32)
        g2T = A.alloc(64, F32)
        pscT = A.alloc(32, F32)
        invf = A.alloc(256, F32, "p (g t) -> p g t", g=4)
        sgwT = A.alloc(2048, BF16, "p (h i) -> p h i", h=8)
        wpool = A.alloc(4096, BF16, "p (g k n) -> p g k n", g=4, k=2)
        A.ptr = (A.ptr + GRAN - 1) // GRAN * GRAN
        S.fine = (A.ptr, A.ptr + 1024)
        stat = A.alloc(4 * 256, F32)
        A.ptr = (A.ptr + GRAN - 1) // GRAN * GRAN
        stat_ctr = [0]

        def st(n):
            o = stat_ctr[0]
            stat_ctr[0] = o + n
            assert stat_ctr[0] <= 256
            return stat[:, o:o + n]

        def dma(eng, out, in_, group=None, after=()):
            return S.add(eng, lambda e, o=out, i=in_: e.dma_start(out=o, in_=i),
                         reads=[in_], writes=[out], dma=True, group=group, after=after)

        dma("sp", ident, ident_d, group="c0")
        dma("sp", g1T, g1T_d, group="c0")
        dma("sp", g2T, g2T_d, group="c0")
        dma("sp", pscT, pscT_d, group="c0")
        dma("sp", invf, invf_d, group="c0")
        dma("pool", sgwT, sgwT_d)
        dma("pool", wpool, w_pool)
        S.add("dve", lambda e: e.memset(sgwT[64:128, :, 0:64], 0.0), reads=[], writes=[sgwT])

        A.ptr = (A.ptr + GRAN - 1) // GRAN * GRAN
        mergedT_base = A.ptr
        mergedT = A.alloc(16 * TOK * 2, BF16, "p (k n) -> p k n", k=16)
        r1 = A.ptr
        hT = A.alloc(16 * TOK * 2, BF16, "p (k n) -> p k n", k=16)
        hTh = A.alloc(16 * HALO * 2, BF16, "p (k n) -> p k n", k=16)
        v_ln = A.alloc(8 * DS * 2, BF16, "p (t d) -> p t d", t=8)
        yAT = A.alloc(8 * TOK * 2, BF16, "p (k n) -> p k n", k=8)
        pooled_base = (A.ptr + GRAN - 1) // GRAN * GRAN
        pooledT = A.alloc(8 * TOK * 2, BF16, "p (k n) -> p k n", k=8)
        ybT = A.alloc(8 * TOK * 2, BF16, "p (k n) -> p k n", k=8)
        r1_end = A.ptr
        A.ptr = r1
        xs = A.alloc(4 * D * 4, F32, "p (t d) -> p t d", t=4)
        h2T = A.alloc(16 * HT * 2, BF16, "p (k n) -> p k n", k=16)
        hid_base = A.ptr
        hidT = A.alloc(NFC * HT * 2, BF16, "p (k n) -> p k n", k=NFC)
        A.ptr = max(A.ptr, r1_end)
        mofo_base = A.ptr
        mo = A.alloc(4 * D * 4, F32, "p (t d) -> p t d", t=4)
        fo = mo
        A.ptr = mofo_base
        Wv = A.alloc(2 * 16 * 512 * 2, BF16, "p (c k n) -> p c k n", c=2, k=16)
        wbase = A.ptr
        wsize = ARENA_BYTES - wbase
        assert wsize >= 38 * 1024, wsize
        mofo_scratch = mofo_base

        def wregion():
            A.ptr = wbase

        def wavefront(stages, items):
            n, m = len(items), len(stages)
            for step in range(n + m - 1):
                for j in reversed(range(m)):
                    i = step - j
                    if 0 <= i < n:
                        stages[j](items[i])

        def rstd_ops(ssq, rs, P):
            S.add("dve", lambda e: e.tensor_scalar(out=rs[0:P, :], in0=ssq[0:P, :], scalar1=1.0 / D,
                                                   scalar2=EPS, op0=ALU.mult, op1=ALU.add),
                  reads=[ssq], writes=[rs])
            S.add("act", lambda e: e.activation(out=rs[0:P, :], in_=rs[0:P, :], func=AF.Sqrt),
                  reads=[rs], writes=[rs])
            S.add("dve", lambda e: e.reciprocal(out=rs[0:P, :], in_=rs[0:P, :]), reads=[rs], writes=[rs])

        def transpose_evac(xb, P, gT, dstT, tok0):
            for half in range(2):
                b = nb()
                pb = PSB(b)

                def tr(e, half=half, pb=pb):
                    ins = None
                    for cc in range(8):
                        c = half * 8 + cc
                        ins = e.transpose(out=pb[:, cc * 128: cc * 128 + P],
                                          in_=xb[0:P, c * 128:(c + 1) * 128], identity=ident[0:P, 0:P])
                    return ins
                S.add("pe", tr, reads=[xb, ident], writes=[PS(b)])
                for cc in range(8):
                    c = half * 8 + cc
                    src_ps = pb[:, cc * 128: cc * 128 + P]
                    dst = dstT[:, c, tok0:tok0 + P]
                    if half == 0:
                        S.add("act", lambda e, s=src_ps, d=dst, c=c: e.activation(
                            out=d, in_=s, func=AF.Copy, scale=gT[:, c:c + 1]),
                            reads=[PS(b), gT], writes=[dst])
                    else:
                        S.add("dve", lambda e, s=src_ps, d=dst, c=c: e.tensor_scalar(
                            out=d, in0=s, scalar1=gT[:, c:c + 1], scalar2=None, op0=ALU.mult),
                            reads=[PS(b), gT], writes=[dst])

        def stage_AB():
            wregion()
            NST = 3
            xst = [A.alloc(D * 4, F32) for _ in range(NST)]
            xnb = [A.alloc(D * 2, BF16) for _ in range(NST)]
            items = []
            for k in range(9):
                P = HALO if k == 0 else 128
                row0 = 0 if k == 0 else HALO + (k - 1) * 128
                items.append(dict(k=k, P=P, row0=row0, src=xst[k % NST][0:P, :], xb=xnb[k % NST],
                                  ssq=st(1), rs=st(1),
                                  dstT=hTh if k == 0 else hT, tok0=0 if k == 0 else (k - 1) * 128))

            def s_load(it):
                it["load"] = dma("sp", it["src"], x_ext[it["row0"]:it["row0"] + it["P"], :])

            def s_sq(it):
                P = it["P"]
                S.add("act", lambda e: e.activation(out=it["xb"][0:P, :], in_=it["src"], func=AF.Square,
                                                    accum_out=it["ssq"][0:P, :]),
                      reads=[it["src"]], writes=[it["xb"], it["ssq"]])

            def s_rstd(it):
                rstd_ops(it["ssq"], it["rs"], it["P"])

            def s_scale(it):
                P = it["P"]
                S.add("dve", lambda e: e.tensor_scalar(out=it["xb"][0:P, :], in0=it["src"],
                                                       scalar1=it["rs"][0:P, :], scalar2=None, op0=ALU.mult),
                      reads=[it["src"], it["rs"]], writes=[it["xb"]])

            def s_tr(it):
                transpose_evac(it["xb"], it["P"], g1T, it["dstT"], it["tok0"])

            wavefront([s_load, s_sq, s_rstd, s_scale, s_tr], items)
            if dbg and "hT" in dbg:
                dma("sp", dbg_d["hT"], hT)
                dma("sp", dbg_d["hTh"], hTh)
            if last_stage == "P0":
                return

            wregion()
            ug = [A.alloc(HT * 4, F32), A.alloc(HT * 4, F32)]
            tmpA = [A.alloc(HT * 4, F32), A.alloc(HT * 4, F32)]
            t16 = A.alloc(HALO * 4, F32)
            sgub = A.alloc(8 * 128 * 4, F32, "p (h i) -> p h i", h=8)
            vg = [A.alloc(DS * 4, F32), A.alloc(DS * 4, F32)]
            lng = A.alloc(DS * 4, F32)
            lnb = A.alloc(DS * 4, F32)
            L = HALO + TOK
            A.ptr = mergedT_base
            NRA = 4
            ringA = [A.alloc(16 * 128 * 2, BF16, "p (k n) -> p k n", k=16) for _ in range(NRA)]
            pT1 = A.alloc(L * 4, F32)
            pT = [pT1, pT1]
            sA = A.alloc(L * 4, F32)
            sB = A.alloc(L * 4, F32)
            assert A.ptr <= mergedT_base + 16 * TOK * 2
            dma("sp", sgub, sgub_d)
            ra = [0]

            def wtile(src, after=()):
                slot = ringA[ra[0] % NRA]
                ra[0] += 1
                dma("pool", slot, src, after=after)
                return slot

            for c in range(8):
                g = c // 2
                w = 2 << g
                slot = wtile(w_p[c], after=[items[6]["load"]] if c in (1, 2, 3) else ())
                if c == 3:
                    for ct in range(2):
                        dma("pool", Wv[:, ct], w_v[ct], after=[items[8]["load"]])
                pj = pT[c % 2]
                bh = nb()

                def mmph(e, slot=slot, bh=bh):
                    ins = None
                    for kc in range(16):
                        ins = e.matmul(PS(bh)[:, 0:HALO], lhsT=slot[:, kc, :], rhs=hTh[:, kc, :],
                                       start=(kc == 0), stop=(kc == 15))
                    return ins
                S.add("pe", mmph, reads=[slot, hTh], writes=[PS(bh)])
                S.add("dve", lambda e, bh=bh, pj=pj: e.tensor_copy(out=pj[:, 0:HALO], in_=PS(bh)[:, 0:HALO]),
                      reads=[PS(bh)], writes=[pj[:, 0:HALO]])
                for hf in range(2):
                    b1 = nb()

                    def mmp(e, slot=slot, b1=b1, hf=hf):
                        ins = None
                        for kc in range(16):
                            ins = e.matmul(PS(b1), lhsT=slot[:, kc, :], rhs=hT[:, kc, hf * HT:(hf + 1) * HT],
                                           start=(kc == 0), stop=(kc == 15))
                        return ins
                    S.add("pe", mmp, reads=[slot, hT[:, :, hf * HT:(hf + 1) * HT]], writes=[PS(b1)])
                    dstp = pj[:, HALO + hf * HT: HALO + (hf + 1) * HT]
                    S.add("act", lambda e, b1=b1, d=dstp: e.activation(out=d, in_=PS(b1), func=AF.Copy),
                          reads=[PS(b1)], writes=[dstp])
                cur = pj
                lo = 0
                step = 1
                bufs = [sA, sB]
                bi = 0
                while step < w:
                    nlo = lo + step
                    dst = bufs[bi]
                    S.add("dve", lambda e, dst=dst, cur=cur, nlo=nlo, step=step: e.tensor_tensor(
                        out=dst[:, nlo:L], in0=cur[:, nlo:L], in1=cur[:, nlo - step:L - step], op=ALU.add),
                        reads=[cur], writes=[dst])
                    cur = dst
                    lo = nlo
                    step *= 2
                    bi ^= 1
                S.add("dve", lambda e, cur=cur, pj=pj, c=c, w=w: e.scalar_tensor_tensor(
                    out=pooledT[:, c, :], in0=cur[:, HALO:], scalar=1.0 / w, in1=pj[:, HALO:],
                    op0=ALU.mult, op1=ALU.subtract),
                    reads=[cur, pj], writes=[pooledT[:, c, :]])
                S.add("dve", lambda e, cur=cur, g=g: e.tensor_tensor(
                    out=t16, in0=cur[:, HALO:2 * HALO], in1=invf[:, g, :], op=ALU.mult),
                    reads=[cur, invf], writes=[t16])
                S.add("dve", lambda e, pj=pj, c=c: e.tensor_tensor(
                    out=pooledT[:, c, 0:HALO], in0=t16, in1=pj[:, HALO:2 * HALO], op=ALU.subtract),
                    reads=[t16, pj], writes=[pooledT[:, c, 0:HALO]])
            if dbg and "pooledT" in dbg:
                dma("sp", dbg_d["pooledT"], pooledT)
            if last_stage == "A3":
                return

            dma("sp", lng, lng_d, group="ln")
            dma("sp", lnb, lnb_d, group="ln")
            for tb in range(8):
                vgj = vg[tb % 2]
                bst = st(12)
                mv = st(3)
                for ct in range(2):
                    b = nb()

                    def mmv(e, tb=tb, ct=ct, b=b):
                        ins = None
                        for kc in range(16):
                            ins = e.matmul(PS(b), lhsT=hT[:, kc, tb * 128:(tb + 1) * 128],
                                           rhs=Wv[:, ct, kc, :], start=(kc == 0), stop=(kc == 15))
                        return ins
                    S.add("pe", mmv, reads=[hT[:, :, tb * 128:(tb + 1) * 128], Wv[:, ct]], writes=[PS(b)])
                    dstv = vgj[:, ct * 512:(ct + 1) * 512]
                    S.add("act", lambda e, b=b, d=dstv: e.activation(out=d, in_=PS(b), func=AF.Gelu),
                          reads=[PS(b)], writes=[dstv])
                    S.add("dve", lambda e, d=dstv, o=bst[:, ct * 6:(ct + 1) * 6]: e.bn_stats(out=o, in_=d),
                          reads=[dstv], writes=[bst[:, ct * 6:(ct + 1) * 6]])
                S.add("dve", lambda e, bst=bst, mv=mv: e.bn_aggr(out=mv[:, 0:2], in_=bst),
                      reads=[bst], writes=[mv])
                S.add("dve", lambda e, mv=mv: e.tensor_scalar(out=mv[:, 2:3], in0=mv[:, 1:2], scalar1=EPS,
                                                              scalar2=None, op0=ALU.add),
                      reads=[mv], writes=[mv])
                S.add("act", lambda e, mv=mv: e.activation(out=mv[:, 2:3], in_=mv[:, 2:3], func=AF.Sqrt),
                      reads=[mv], writes=[mv])
                S.add("dve", lambda e, mv=mv: e.reciprocal(out=mv[:, 2:3], in_=mv[:, 2:3]),
                      reads=[mv], writes=[mv])
                S.add("dve", lambda e, v=vgj, mv=mv: e.tensor_scalar(out=v, in0=v, scalar1=mv[:, 0:1],
                                                                     scalar2=mv[:, 2:3], op0=ALU.subtract,
                                                                     op1=ALU.mult),
                      reads=[vgj, mv], writes=[vgj])
                S.add("dve", lambda e, v=vgj: e.tensor_tensor(out=v, in0=v, in1=lng, op=ALU.mult),
                      reads=[vgj, lng], writes=[vgj])
                S.add("dve", lambda e, v=vgj, tb=tb: e.tensor_tensor(out=v_ln[:, tb, :], in0=v, in1=lnb,
                                                                    op=ALU.add),
                      reads=[vgj, lnb], writes=[v_ln[:, tb, :]])
            if dbg and "v_ln" in dbg:
                dma("sp", dbg_d["v_ln"], v_ln)
            if last_stage == "A1":
                return

            for m in range(8):
                g = m // 2
                for hf in range(2):
                    b = nb()

                    def mmb(e, m=m, g=g, b=b, hf=hf):
                        ins = None
                        for kc in range(2):
                            ins = e.matmul(PS(b), lhsT=wpool[:, g, kc, (m % 2) * 128:(m % 2 + 1) * 128],
                                           rhs=pooledT[:, 2 * g + kc, hf * HT:(hf + 1) * HT],
                                           start=(kc == 0), stop=(kc == 1))
                        return ins
                    S.add("pe", mmb, reads=[wpool, pooledT[:, 2 * g:2 * g + 2, hf * HT:(hf + 1) * HT]],
                          writes=[PS(b)])
                    dsty = ybT[:, m, hf * HT:(hf + 1) * HT]
                    S.add("act", lambda e, m=m, b=b, d=dsty: e.activation(out=d, in_=PS(b), func=AF.Copy,
                                                                        scale=pscT[:, m:m + 1]),
                          reads=[PS(b), pscT], writes=[dsty])
            if dbg and "ybT" in dbg:
                dma("sp", dbg_d["ybT"], ybT)
            if last_stage == "A4":
                return

            for h in range(8):
                slot = wtile(w_u[h])
                for hf in range(2):
                    b1 = nb()

                    def mmu(e, slot=slot, b1=b1, hf=hf):
                        ins = None
                        for kc in range(16):
                            ins = e.matmul(PS(b1), lhsT=slot[:, kc, :], rhs=hT[:, kc, hf * HT:(hf + 1) * HT],
                                           start=(kc == 0), stop=(kc == 15))
                        return ins
                    S.add("pe", mmu, reads=[slot, hT[:, :, hf * HT:(hf + 1) * HT]], writes=[PS(b1)])
                    ugj = ug[hf]
                    S.add("act", lambda e, b1=b1, u=ugj: e.activation(out=u, in_=PS(b1), func=AF.Gelu),
                          reads=[PS(b1)], writes=[ugj])
                    b2 = nb()

                    def mmx(e, h=h, b2=b2, hf=hf):
                        ins = None
                        for t in range(4):
                            tb = hf * 4 + t
                            ins = e.matmul(PS(b2)[:, t * 128:(t + 1) * 128],
                                           lhsT=v_ln[:, tb, h * 128:(h + 1) * 128], rhs=sgwT[:, h, :],
                                           start=True, stop=True)
                        return ins
                    S.add("pe", mmx, reads=[v_ln[:, hf * 4:(hf + 1) * 4, h * 128:(h + 1) * 128], sgwT],
                          writes=[PS(b2)])
                    tj = tmpA[hf]
                    S.add("dve", lambda e, b2=b2, tj=tj, h=h: e.tensor_tensor(
                        out=tj.rearrange("p (t i) -> p t i", t=4),
                        in0=PS(b2).rearrange("p (t i) -> p t i", t=4),
                        in1=sgub[:, h:h + 1, :].to_broadcast([128, 4, 128]), op=ALU.add),
                        reads=[PS(b2), sgub], writes=[tj])
                    dsta = yAT[:, h, hf * HT:(hf + 1) * HT]
                    S.add("dve", lambda e, tj=tj, u=ugj, d=dsta: e.tensor_tensor(out=d, in0=tj, in1=u,
                                                                              op=ALU.mult),
                          reads=[tj, ugj], writes=[dsta])
            if dbg and "yAT" in dbg:
                dma("sp", dbg_d["yAT"], yAT)
            if last_stage == "A2":
                return

            wregion()
            ringB2 = A.alloc(48 * 128 * 2, BF16, "p (k n) -> p k n", k=48)
            A.ptr = mofo_scratch
            ringB = [A.alloc(48 * 128 * 2, BF16, "p (k n) -> p k n", k=48) for _ in range(2)] + [ringB2]
            sa = [A.alloc(HT * 4, F32), A.alloc(HT * 4, F32)]
            sb = [A.alloc(HT * 4, F32), A.alloc(HT * 4, F32)]
            assert A.ptr <= mofo_scratch + 4 * D * 4
            for d in range(16):
                slot = ringB[d % 3]
                dma("pool", slot, w_gab[d])
                for hf in range(2):
                    tk = slice(hf * HT, (hf + 1) * HT)
                    bs = [hf * 4 + i for i in range(4)]

                    def mmg(e, slot=slot, bs=bs, tk=tk):
                        ins = None
                        for kc in range(16):
                            ins = e.matmul(PS(bs[0]), lhsT=slot[:, kc, :], rhs=hT[:, kc, tk],
                                           start=(kc == 0), stop=(kc == 15))
                        for kc in range(16):
                            ins = e.matmul(PS(bs[1]), lhsT=slot[:, 16 + kc, :], rhs=hT[:, kc, tk],
                                           start=(kc == 0), stop=(kc == 15))
                        for kc in range(8):
                            ins = e.matmul(PS(bs[2]), lhsT=slot[:, 32 + kc, :], rhs=yAT[:, kc, tk],
                                           start=(kc == 0), stop=(kc == 7))
                        for kc in range(8):
                            ins = e.matmul(PS(bs[3]), lhsT=slot[:, 40 + kc, :], rhs=ybT[:, kc, tk],
                                           start=(kc == 0), stop=(kc == 7))
                        return ins
                    S.add("pe", mmg, reads=[slot, hT[:, :, tk], yAT[:, :, tk], ybT[:, :, tk]],
                          writes=[PS(b) for b in bs])
                    saj, sbj = sa[hf], sb[hf]
                    S.add("act", lambda e, b=bs[0], o=saj: e.activation(out=o, in_=PS(b), func=AF.Sigmoid),
                          reads=[PS(bs[0])], writes=[saj])
                    S.add("act", lambda e, b=bs[1], o=sbj: e.activation(out=o, in_=PS(b), func=AF.Sigmoid),
                          reads=[PS(bs[1])], writes=[sbj])
                    S.add("dve", lambda e, b=bs[2], o=saj: e.tensor_tensor(out=o, in0=o, in1=PS(b), op=ALU.mult),
                          reads=[saj, PS(bs[2])], writes=[saj])
                    S.add("dve", lambda e, b=bs[3], o=sbj: e.tensor_tensor(out=o, in0=o, in1=PS(b), op=ALU.mult),
                          reads=[sbj, PS(bs[3])], writes=[sbj])
                    dstm = mergedT[:, d, tk]
                    S.add("dve", lambda e, a=saj, bb=sbj, dd=dstm: e.tensor_tensor(out=dd, in0=a, in1=bb,
                                                                                 op=ALU.add),
                          reads=[saj, sbj], writes=[dstm])
            if dbg and "mergedT" in dbg:
                dma("sp", dbg_d["mergedT"], mergedT)

        def stage_CDE(hf):
            t0 = hf * HT
            tk0 = hf * HT
            wregion()
            NRD = 3
            ringD = [A.alloc(32 * 128 * 2, BF16, "p (k n) -> p k n", k=32) for _ in range(NRD)]
            sg = [A.alloc(HT * 4, F32), A.alloc(HT * 4, F32)]
            gp_off = A.ptr
            A.ptr = hid_base
            ringC_a = A.alloc(16 * 512 * 2, BF16, "p (k n) -> p k n", k=16)
            A.ptr = pooled_base
            ringC_b = A.alloc(16 * 512 * 2, BF16, "p (k n) -> p k n", k=16)
            ringC = [ringC_a, ringC_b]
            xnb = [A.alloc(D * 2, BF16) for _ in range(3)]
            assert A.ptr <= r1_end, (A.ptr, r1_end)
            A.ptr = wbase + 12 * 1024
            ringC0 = A.alloc(16 * 512 * 2, BF16, "p (k n) -> p k n", k=16)
            A.ptr = gp_off
            gp = A.alloc(D * 4, F32)
            c_parts = [st(4) for _ in range(4)]
            ringC3 = h2T.rearrange("p k n -> p (k n)").rearrange("p (k n) -> p k n", k=16)
            slots = [ringC0 if hf == 0 else ringC[0], ringC[1] if hf == 0 else mergedT[:, :, 0:HT], ringC[0], ringC3]
            dma("pool", slots[0], w_mix[0])
            for q_ in range(3):
                dma("pool", gp[:, q_ * 512:(q_ + 1) * 512], gp1_d[:, q_ * 512:(q_ + 1) * 512])
            op_et1 = dma("pool", slots[1], w_mix[1])
            for tb in range(4):
                dma("sp", xs[:, tb, :], x_ext[HALO + t0 + tb * 128: HALO + t0 + (tb + 1) * 128, :],
                    after=[op_et1])
            dma("pool", gp[:, 3 * 512:4 * 512], gp1_d[:, 3 * 512:4 * 512])
            dma("pool", slots[3], w_mix[3])

            def mm_ev(et, tb, defer=None):
                slot = slots[et]
                b = nb()

                def mmc(e):
                    ins = None
                    for kc in range(16):
                        ins = e.matmul(PS(b), lhsT=mergedT[:, kc, tk0 + tb * 128: tk0 + (tb + 1) * 128],
                                       rhs=slot[:, kc, :], start=(kc == 0), stop=(kc == 15))
                    return ins
                S.add("pe", mmc, reads=[slot, mergedT[:, :, tk0 + tb * 128: tk0 + (tb + 1) * 128]],
                      writes=[PS(b)])
                dst = mo[:, tb, et * 512:(et + 1) * 512]

                def ev():
                    junk = xnb[(et * 4 + tb) % 3][:, 0:512]
                    part = c_parts[tb][:, et:et + 1]
                    S.add("act", lambda e: e.activation(out=junk, in_=PS(b), func=AF.Square, accum_out=part),
                          reads=[PS(b)], writes=[junk, part])
                    g_ = gp[:, et * 512:(et + 1) * 512]
                    S.add("dve", lambda e: e.tensor_tensor(out=dst, in0=PS(b), in1=g_, op=ALU.mult),
                          reads=[PS(b), g_], writes=[dst])
                if defer is not None:
                    defer.append(ev)
                else:
                    ev()

            evacs = []
            for tb in range(4):
                mm_ev(0, tb, defer=evacs)
            for ev_ in evacs:
                ev_()
            dma("pool", slots[2], w_mix[2])
            for tb in range(4):
                mm_ev(1, tb)
            mm_ev(2, 0)
            mm_ev(2, 1)
            c_mm = {0: lambda: mm_ev(3, 0), 1: lambda: mm_ev(3, 1),
                    2: lambda: (mm_ev(2, 2), mm_ev(3, 2)), 3: lambda: (mm_ev(2, 3), mm_ev(3, 3))}
            items = [dict(tb=tb, xb=xnb[tb % 3], ssq=st(1), rs=st(1), ssq2=st(1), rs2=st(1)) for tb in range(4)]

            def c_sq(it, parts=c_parts):
                p_ = parts[it["tb"]]
                S.add("dve", lambda e: e.tensor_reduce(out=it["ssq"], in_=p_, axis=mybir.AxisListType.X, op=ALU.add),
                      reads=[p_], writes=[it["ssq"]])

            def c_rstd(it):
                rstd_ops(it["ssq"], it["rs"], 128)

            def c_res(it, src=mo):
                tb = it["tb"]
                s_ = src[:, tb, :]
                x_ = xs[:, tb, :]
                S.add("dve", lambda e: e.scalar_tensor_tensor(out=x_, in0=s_, scalar=it["rs"], in1=x_,
                                                              op0=ALU.mult, op1=ALU.add),
                      reads=[s_, it["rs"], x_], writes=[x_])

            def c_sq2(it):
                s_ = xs[:, it["tb"], :]
                S.add("act", lambda e: e.activation(out=it["xb"], in_=s_, func=AF.Square, accum_out=it["ssq2"]),
                      reads=[s_], writes=[it["xb"], it["ssq2"]])

            def c_rstd2(it):
                rstd_ops(it["ssq2"], it["rs2"], 128)

            def c_scale(it):
                s_ = xs[:, it["tb"], :]
                S.add("dve", lambda e: e.tensor_scalar(out=it["xb"], in0=s_, scalar1=it["rs2"], scalar2=None,
                                                       op0=ALU.mult),
                      reads=[s_, it["rs2"]], writes=[it["xb"]])

            def c_tr(it):
                transpose_evac(it["xb"], 128, g2T, h2T, it["tb"] * 128)

            def d_sub(f, lo, hi):
                slot = ringD[f % NRD]
                b1, b2 = nb(), nb()

                def mmd(e):
                    ins = None
                    for kc in range(16):
                        ins = e.matmul(PS(b1)[:, lo:hi], lhsT=slot[:, kc, :], rhs=h2T[:, kc, lo:hi],
                                       start=(kc == 0), stop=(kc == 15))
                    for kc in range(16):
                        ins = e.matmul(PS(b2)[:, lo:hi], lhsT=slot[:, 16 + kc, :], rhs=h2T[:, kc, lo:hi],
                                       start=(kc == 0), stop=(kc == 15))
                    return ins
                S.add("pe", mmd, reads=[slot, h2T[:, :, lo:hi]], writes=[PS(b1), PS(b2)])
                sgj = sg[f % 2]
                S.add("act", lambda e: e.activation(out=sgj[:, lo:hi], in_=PS(b1)[:, lo:hi], func=AF.Silu),
                      reads=[PS(b1)], writes=[sgj[:, lo:hi]])
                S.add("dve", lambda e: e.tensor_tensor(out=hidT[:, f, lo:hi], in0=sgj[:, lo:hi],
                                                       in1=PS(b2)[:, lo:hi], op=ALU.mult),
                      reads=[sgj[:, lo:hi], PS(b2)], writes=[hidT[:, f, lo:hi]])

            NSUB = NRD if last_stage not in ("C",) else 0
            for f in range(NSUB):
                dma("pool", ringD[f % NRD], w_gu[f])

            def c_dA(it):
                if it["tb"] == 1:
                    for f in range(NSUB):
                        d_sub(f, 0, 256)

            wavefront([lambda it: c_mm[it["tb"]](), c_sq, c_rstd, c_res, c_sq2, c_rstd2, c_scale, c_tr, c_dA],
                      items)
            for f in range(NSUB):
                d_sub(f, 256, 512)
            if dbg and "x1" in dbg and hf == dbg.get("_hf", 0):
                dma("sp", dbg_d["x1"].rearrange("(t p) d -> p t d", p=128), xs)
                dma("sp", dbg_d["h2T"], h2T)
            if last_stage == "C":
                return

            for f in range(NSUB, NFC):
                slot = ringD[f % NRD]
                dma("pool", slot, w_gu[f])
                b1, b2 = nb(), nb()

                def mmd(e, slot=slot, b1=b1, b2=b2):
                    ins = None
                    for kc in range(16):
                        ins = e.matmul(PS(b1), lhsT=slot[:, kc, :], rhs=h2T[:, kc, :],
                                       start=(kc == 0), stop=(kc == 15))
                    for kc in range(16):
                        ins = e.matmul(PS(b2), lhsT=slot[:, 16 + kc, :], rhs=h2T[:, kc, :],
                                       start=(kc == 0), stop=(kc == 15))
                    return ins
                S.add("pe", mmd, reads=[slot, h2T], writes=[PS(b1), PS(b2)])
                sgj = sg[f % 2]
                S.add("act", lambda e, b1=b1, o=sgj: e.activation(out=o, in_=PS(b1), func=AF.Silu),
                      reads=[PS(b1)], writes=[sgj])
                S.add("dve", lambda e, b2=b2, o=sgj, f=f: e.tensor_tensor(out=hidT[:, f, :], in0=o, in1=PS(b2),
                                                                         op=ALU.mult),
                      reads=[sgj, PS(b2)], writes=[hidT[:, f, :]])
            if last_stage == "D":
                return

            wregion()
            NRE = 7
            ringE = [A.alloc(4 * 512 * 2, BF16, "p (k n) -> p k n", k=4) for _ in range(NRE)]
            assert A.ptr <= gp_off
            A.ptr = gp_off
            gp2 = A.alloc(D * 4, F32)
            xne = [A.alloc(512 * 2, BF16) for _ in range(2)]
            assert A.ptr <= wbase + wsize
            dma("pool", gp2, gp2_d)
            e_parts = [st(4) for _ in range(4)]
            re_ = 0
            for et in range(4):
                banks = [(et % 2) * 4 + tb for tb in range(4)]
                for fg in range(11):
                    slot = ringE[re_ % NRE]
                    re_ += 1
                    dma("pool", slot, w_dn[et, fg])

                    def mme(e, slot=slot, fg=fg, banks=banks):
                        ins = None
                        for fl in range(4):
                            fc = fg * 4 + fl
                            for tb in range(4):
                                ins = e.matmul(PS(banks[tb]), lhsT=hidT[:, fc, tb * 128:(tb + 1) * 128],
                                               rhs=slot[:, fl, :], start=(fc == 0), stop=(fc == NFC - 1))
                        return ins
                    S.add("pe", mme, reads=[slot, hidT[:, fg * 4:(fg + 1) * 4, :]], writes=[PS(b) for b in banks])
                for tb in range(4):
                    b = banks[tb]
                    dst = fo[:, tb, et * 512:(et + 1) * 512]
                    junk = xne[tb % 2]
                    part = e_parts[tb][:, et:et + 1]
                    S.add("act", lambda e, b=b, j=junk, p=part: e.activation(out=j, in_=PS(b), func=AF.Square,
                                                                             accum_out=p),
                          reads=[PS(b)], writes=[junk, part])
                    g_ = gp2[:, et * 512:(et + 1) * 512]
                    S.add("dve", lambda e, b=b, d=dst, g_=g_: e.tensor_tensor(out=d, in0=PS(b), in1=g_, op=ALU.mult),
                          reads=[PS(b), g_], writes=[dst])
            items = [dict(tb=tb, xb=xne[tb % 2], ssq=st(1), rs=st(1)) for tb in range(4)]

            def e_out(it):
                tb = it["tb"]
                o_ = out_d[t0 + tb * 128: t0 + (tb + 1) * 128, :]
                S.add("sp", lambda e: e.dma_start(out=o_, in_=xs[:, tb, :]),
                      reads=[xs[:, tb, :]], writes=[("dram", "out", tb)], dma=True)

            wavefront([lambda it: c_sq(it, parts=e_parts), c_rstd, lambda it: c_res(it, src=fo), e_out], items)

        stage_AB()
        if last_stage in (None, "C", "D", "E"):
            for hf in range(nhalves):
                stat_ctr[0] = 128
                stage_CDE(hf)

        fin = S.add("sp", None, reads=[out_d] + list(dbg_d.values()), writes=[])

        S.analyze()
        outnames = {("dram", "out")} | {("dram", "dbg_" + n) for n in dbg_d}
        for i, op in enumerate(S.ops):
            if op.dma and any((w in outnames or w[:2] == ("dram", "out")) for w in op.W) and i not in fin.deps:
                fin.deps.append(i)

        def mksem(name):
            h = es.enter_context(nc.semaphore("s" + name))
            sems.append(h)
            return h

        run = S.emit(mksem)
        with nc.Block() as block:
            @block.sync
            def _(e):
                run("sp", e)

            @block.gpsimd
            def _(e):
                run("pool", e)

            @block.tensor
            def _(e):
                run("pe", e)

            @block.scalar
            def _(e):
                run("act", e)

            @block.vector
            def _(e):
                run("dve", e)
    return nc, S


def _tile_cols(W, ncol):
    K, N = W.shape
    return np.ascontiguousarray(W.reshape(K // 128, 128, N // ncol, ncol).transpose(2, 1, 0, 3))


def prepare_inputs(x, norm1_pre, w_in, v_ln_g, v_ln_b, sgu_w, sgu_b, pool_w, pool_scale,
                   w_a_out, w_b_out, w_mix_out, norm1_post, norm2_pre, w_ffn_gate,
                   w_ffn_up, w_ffn_down, norm2_post):
    f32 = np.float32
    x = np.asarray(x, f32)
    W_in = np.asarray(w_in, f32)[0]
    shared = {}
    shared["w_u"] = _tile_cols(W_in[:, 0:1024], 128)
    shared["w_v"] = _tile_cols(W_in[:, 1024:2048], 512)
    shared["w_p"] = _tile_cols(W_in[:, 2048:3072], 128)
    ga = _tile_cols(W_in[:, 3072:5120], 128)
    gb = _tile_cols(W_in[:, 5120:7168], 128)
    wa = _tile_cols(np.asarray(w_a_out, f32)[0], 128)
    wb = _tile_cols(np.asarray(w_b_out, f32)[0], 128)
    shared["w_gab"] = np.ascontiguousarray(np.concatenate([ga, gb, wa, wb], axis=2))
    pw = np.asarray(pool_w, f32)[0]
    shared["w_pool"] = np.ascontiguousarray(pw.reshape(4, 2, 128, 256).transpose(2, 0, 1, 3))
    shared["w_mix"] = _tile_cols(np.asarray(w_mix_out, f32)[0], 512)
    wg = _tile_cols(np.asarray(w_ffn_gate, f32)[0], 128)
    wu = _tile_cols(np.asarray(w_ffn_up, f32)[0], 128)
    shared["w_gu"] = np.ascontiguousarray(np.concatenate([wg, wu], axis=2))
    wd = np.asarray(w_ffn_down, f32)[0]
    shared["w_dn"] = np.ascontiguousarray(wd.reshape(11, 4, 128, 4, 512).transpose(3, 0, 2, 1, 4))
    sw = np.asarray(sgu_w, f32)[0]
    shared["sgwT"] = np.ascontiguousarray(sw.transpose(2, 0, 1))
    shared["sgub"] = np.ascontiguousarray(np.broadcast_to(np.asarray(sgu_b, f32)[0][None], (128, 8, 128)))
    shared["lng"] = np.ascontiguousarray(np.broadcast_to(np.asarray(v_ln_g, f32)[0][None], (128, DS)))
    shared["lnb"] = np.ascontiguousarray(np.broadcast_to(np.asarray(v_ln_b, f32)[0][None], (128, DS)))
    shared["gp1"] = np.ascontiguousarray(np.broadcast_to(np.asarray(norm1_post, f32)[0][None], (128, D)))
    shared["gp2"] = np.ascontiguousarray(np.broadcast_to(np.asarray(norm2_post, f32)[0][None], (128, D)))
    shared["g1T"] = np.ascontiguousarray(np.asarray(norm1_pre, f32)[0].reshape(16, 128).T)
    shared["g2T"] = np.ascontiguousarray(np.asarray(norm2_pre, f32)[0].reshape(16, 128).T)
    shared["pscT"] = np.ascontiguousarray(np.asarray(pool_scale, f32)[0].reshape(8, 128).T)
    shared["ident"] = np.eye(128, dtype=f32).astype(ml_dtypes.bfloat16)
    inv_std = np.zeros((4, 16), f32)
    inv_first = np.zeros((4, 16), f32)
    for g in range(4):
        w = 2 << g
        inv_std[g, :] = 1.0 / w
        inv_first[g, :] = 1.0 / np.minimum(np.arange(1, 17), w)
    in_maps = []
    S = x.shape[1]
    cores_per_b = S // TOK
    for c in range(NCORES):
        b, q = divmod(c, cores_per_b)
        t0 = q * TOK
        xe = np.zeros((HALO + TOK, D), f32)
        xe[HALO:] = x[b, t0:t0 + TOK]
        if q > 0:
            xe[:HALO] = x[b, t0 - HALO:t0]
        m = dict(shared)
        m["x_ext"] = xe
        iv = inv_first if q == 0 else inv_std
        m["invf"] = np.ascontiguousarray(np.broadcast_to(iv[None], (128, 4, 16)))
        in_maps.append(m)
    return in_maps


_CACHE = {}


def kernel(**inputs):
    in_maps = prepare_inputs(**inputs)
    if "nc" not in _CACHE:
        _CACHE["nc"] = build_program()[0]
    nc = _CACHE["nc"]
    res = run_bass_kernel_spmd(nc, in_maps, core_ids=list(range(NCORES)))
    x = inputs["x"]
    B, S, _ = x.shape
    out = np.empty((B, S, D), np.float32)
    cores_per_b = S // TOK
    for c in range(NCORES):
        b, q = divmod(c, cores_per_b)
        out[b, q * TOK:(q + 1) * TOK] = res.results[c]["out"]
    return out
```

```python
import numpy as np
import ml_dtypes
import concourse.bass as bass
import concourse.mybir as mybir
from concourse.bass_utils import run_bass_kernel_spmd

F32 = mybir.dt.float32
BF16 = mybir.dt.bfloat16
U8 = mybir.dt.uint8
AF = mybir.ActivationFunctionType
ALU = mybir.AluOpType
ESZ = {F32: 4, BF16: 2, U8: 1}

D = 2048
DS = 1024
DFF = 5632
NFC = DFF // 128
EPS = 1e-6
NCORES = 8
TOK = 1024
HT = 512
HALO = 16
ARENA_BYTES = 212736
GRAN = 256


class Op:
    __slots__ = ("eng", "fn", "R", "W", "dma", "group", "deps", "signal", "sem", "sigval", "key", "after", "idx")

    def __init__(self, eng, fn, R, W, dma, group):
        self.eng, self.fn, self.R, self.W, self.dma, self.group = eng, fn, R, W, dma, group
        self.deps = []
        self.signal = False
        self.sem = None
        self.sigval = 0
        self.key = None
        self.after = []
        self.idx = -1


class Sched:
    def __init__(self, nc):
        self.nc = nc
        self.ops = []

    def res(self, x):
        if not hasattr(x, "tensor"):
            return [x]
        name = x.tensor.name
        es = ESZ[x.dtype]
        if name == "arena":
            g, sp = GRAN, "sb"
            fine = getattr(self, "fine", None)
        elif name == "psum":
            g, sp = 2048, "ps"
        else:
            return [("dram", name)]
        pstride = x.ap[0][0]
        off = x.offset % pstride if pstride else x.offset
        dims = [(st, cnt) for st, cnt in x.ap[1:] if cnt > 1]
        span = 1
        for st, cnt in dims:
            span += (cnt - 1) * abs(st)
        lo = off * es
        hi = (off + span) * es
        if sp == "sb" and fine and lo >= fine[0] and hi <= fine[1]:
            return [("st", i) for i in range(lo // 4, (hi - 1) // 4 + 1)]
        if dims and dims[-1][0] == 1:
            run = dims[-1][1]
            outer = dims[:-1]
        else:
            run, outer = 1, dims
        nruns = 1
        for st, cnt in outer:
            nruns *= cnt
        if outer and nruns <= 512 and all(st >= 0 for st, _ in outer):
            out = set()
            offs = [0]
            for st, cnt in outer:
                offs = [o + st * i for o in offs for i in range(cnt)]
            for o in offs:
                a = (off + o) * es
                b = (off + o + run) * es
                out.update(range(a // g, (b - 1) // g + 1))
            return [(sp, i) for i in sorted(out)]
        return [(sp, i) for i in range(lo // g, (hi - 1) // g + 1)]

    def add(self, eng, fn, reads=(), writes=(), dma=False, group=None, after=()):
        R, W = [], []
        for x in reads:
            R.extend(self.res(x))
        for x in writes:
            W.extend(self.res(x))
        op = Op(eng, fn, R, W, dma, group)
        if dma:
            op.key = group if group is not None else ("k", eng) + tuple(W[0])
        op.after = list(after)
        op.idx = len(self.ops)
        self.ops.append(op)
        return op

    def analyze(self):
        last_w, readers = {}, {}
        for i, op in enumerate(self.ops):
            deps = {}
            for r in op.R:
                w = last_w.get(r)
                if w is not None:
                    deps.setdefault(w, set()).add("RAW")
                if r[0] == "ps":
                    for rd in readers.get(r, ()):
                        if self.ops[rd].eng != op.eng:
                            deps.setdefault(rd, set()).add("PSRAR")
            for r in op.W:
                w = last_w.get(r)
                if w is not None:
                    deps.setdefault(w, set()).add("WAW")
                for rd in readers.get(r, ()):
                    deps.setdefault(rd, set()).add("WAR")
            for r in op.W:
                last_w[r] = i
                readers[r] = []
            for r in op.R:
                lst = readers.setdefault(r, [])
                if not lst or lst[-1] != i:
                    lst.append(i)
            for a in op.after:
                deps.setdefault(a.idx, set()).add("RAW")
            for d, kinds in deps.items():
                if d == i:
                    continue
                dop = self.ops[d]
                if dop.dma and op.dma and dop.group is not None and dop.group == op.group:
                    need = False
                elif dop.dma or op.dma:
                    need = True
                elif dop.eng == op.eng:
                    need = op.eng != "pe" and bool(kinds - {"PSRAR"})
                else:
                    need = True
                if need:
                    op.deps.append(d)
                    dop.signal = True

    def emit(self, sems_ctx):
        nc = self.nc
        cnt = {}
        group_total = {}
        for op in self.ops:
            if op.dma:
                op.signal = True
                k = ("dma", op.key)
                cnt[k] = cnt.get(k, 0) + 16
                op.sigval = cnt[k]
                op.sem = k
                if op.group is not None:
                    group_total[k] = cnt[k]
            elif op.signal:
                k = ("eng", op.eng)
                cnt[k] = cnt.get(k, 0) + 1
                op.sigval = cnt[k]
                op.sem = k
        for op in self.ops:
            if op.dma and op.group is not None:
                op.sigval = group_total[op.sem]
        semh = {}
        for k in cnt:
            semh[k] = sems_ctx(str(len(semh)))
        self.nsems = len(semh)
        per_eng = {}
        for op in self.ops:
            per_eng.setdefault(op.eng, []).append(op)

        def run(engname, e):
            waited = {}
            for op in per_eng.get(engname, []):
                need = {}
                for d in op.deps:
                    dop = self.ops[d]
                    if need.get(dop.sem, 0) < dop.sigval:
                        need[dop.sem] = dop.sigval
                for s, v in need.items():
                    if waited.get(s, 0) < v:
                        e.wait_ge(semh[s], v)
                        waited[s] = v
                if op.fn is None:
                    continue
                ins = op.fn(e)
                if op.signal:
                    ins.then_inc(semh[op.sem], 16 if op.dma else 1)
        return run


class Arena:
    def __init__(self, ap):
        self.ap = ap
        self.ptr = 0

    def alloc(self, nbytes, dtype, pattern=None, **kw):
        off = (self.ptr + GRAN - 1) // GRAN * GRAN if nbytes >= GRAN else (self.ptr + 31) // 32 * 32
        self.ptr = off + nbytes
        assert self.ptr <= ARENA_BYTES, f"arena overflow {self.ptr}"
        v = self.ap[:, off:off + nbytes].bitcast(dtype)
        if pattern:
            v = v.rearrange(pattern, **kw)
        return v


def build_program(dbg=None, nhalves=2, last_stage=None):
    nc = bass.Bass("TRN2", target_bir_lowering=False)
    dram = {}

    def din(name, shape, dt=F32):
        dram[name] = nc.dram_tensor(name, list(shape), dt, kind="ExternalInput").ap()
        return dram[name]

    x_ext = din("x_ext", [HALO + TOK, D])
    w_v = din("w_v", [2, 128, 16, 512])
    w_u = din("w_u", [8, 128, 16, 128])
    w_p = din("w_p", [8, 128, 16, 128])
    w_gab = din("w_gab", [16, 128, 48, 128])
    w_pool = din("w_pool", [128, 4, 2, 256])
    w_mix = din("w_mix", [4, 128, 16, 512])
    w_gu = din("w_gu", [NFC, 128, 32, 128])
    w_dn = din("w_dn", [4, 11, 128, 4, 512])
    sgwT_d = din("sgwT", [128, 8, 128])
    sgub_d = din("sgub", [128, 8, 128])
    lng_d = din("lng", [128, DS])
    lnb_d = din("lnb", [128, DS])
    gp1_d = din("gp1", [128, D])
    gp2_d = din("gp2", [128, D])
    g1T_d = din("g1T", [128, 16])
    g2T_d = din("g2T", [128, 16])
    pscT_d = din("pscT", [128, 8])
    invf_d = din("invf", [128, 4, 16])
    ident_d = din("ident", [128, 128], BF16)
    out_d = nc.dram_tensor("out", [TOK, D], F32, kind="ExternalOutput").ap()
    dbg_d = {}
    if dbg:
        for name, (shape, dt) in dbg.items():
            dbg_d[name] = nc.dram_tensor("dbg_" + name, list(shape), dt, kind="ExternalOutput").ap()

    sems = []
    import contextlib
    with contextlib.ExitStack() as es:
        arena_t = es.enter_context(nc.sbuf_tensor("arena", [128, ARENA_BYTES], U8))
        psum_t = es.enter_context(nc.psum_tensor("psum", [128, 8, 512], F32))
        A = Arena(arena_t)
        S = Sched(nc)

        def PS(b):
            return psum_t[:, b, :]

        def PSB(b):
            return psum_t[:, b, :].bitcast(BF16)

        bank_ctr = [0]

        def nb():
            b = bank_ctr[0]
            bank_ctr[0] = (b + 1) % 8
            return b

        ident = A.alloc(256, BF16)
        g1T = A.alloc(64, F32)
        g2T = A.alloc(64, F32)
        pscT = A.alloc(32, F32)
        invf = A.alloc(256, F32, "p (g t) -> p g t", g=4)
        sgwT = A.alloc(2048, BF16, "p (h i) -> p h i", h=8)
        wpool = A.alloc(4096, BF16, "p (g k n) -> p g k n", g=4, k=2)
        A.ptr = (A.ptr + GRAN - 1) // GRAN * GRAN
        S.fine = (A.ptr, A.ptr + 1024)
        stat = A.alloc(4 * 256, F32)
        A.ptr = (A.ptr + GRAN - 1) // GRAN * GRAN
        stat_ctr = [0]

        def st(n):
            o = stat_ctr[0]
            stat_ctr[0] = o + n
            assert stat_ctr[0] <= 256
            return stat[:, o:o + n]

        def dma(eng, out, in_, group=None, after=()):
            return S.add(eng, lambda e, o=out, i=in_: e.dma_start(out=o, in_=i),
                         reads=[in_], writes=[out], dma=True, group=group, after=after)

        dma("sp", ident, ident_d, group="c0")
        dma("sp", g1T, g1T_d, group="c0")
        dma("sp", g2T, g2T_d, group="c0")
        dma("sp", pscT, pscT_d, group="c0")
        dma("sp", invf, invf_d, group="c0")
        dma("pool", sgwT, sgwT_d)
        dma("pool", wpool, w_pool)
        S.add("dve", lambda e: e.memset(sgwT[64:128, :, 0:64], 0.0), reads=[], writes=[sgwT])

        A.ptr = (A.ptr + GRAN - 1) // GRAN * GRAN
        mergedT_base = A.ptr
        mergedT = A.alloc(16 * TOK * 2, BF16, "p (k n) -> p k n", k=16)
        r1 = A.ptr
        hT = A.alloc(16 * TOK * 2, BF16, "p (k n) -> p k n", k=16)
        hTh = A.alloc(16 * HALO * 2, BF16, "p (k n) -> p k n", k=16)
        v_ln = A.alloc(8 * DS * 2, BF16, "p (t d) -> p t d", t=8)
        yAT = A.alloc(8 * TOK * 2, BF16, "p (k n) -> p k n", k=8)
        pooled_base = (A.ptr + GRAN - 1) // GRAN * GRAN
        pooledT = A.alloc(8 * TOK * 2, BF16, "p (k n) -> p k n", k=8)
        ybT = A.alloc(8 * TOK * 2, BF16, "p (k n) -> p k n", k=8)
        r1_end = A.ptr
        A.ptr = r1
        xs = A.alloc(4 * D * 4, F32, "p (t d) -> p t d", t=4)
        h2T = A.alloc(16 * HT * 2, BF16, "p (k n) -> p k n", k=16)
        hid_base = A.ptr
        hidT = A.alloc(NFC * HT * 2, BF16, "p (k n) -> p k n", k=NFC)
        A.ptr = max(A.ptr, r1_end)
        mofo_base = A.ptr
        mo = A.alloc(4 * D * 4, F32, "p (t d) -> p t d", t=4)
        fo = mo
        A.ptr = mofo_base
        Wv = A.alloc(2 * 16 * 512 * 2, BF16, "p (c k n) -> p c k n", c=2, k=16)
        wbase = A.ptr
        wsize = ARENA_BYTES - wbase
        assert wsize >= 38 * 1024, wsize
        mofo_scratch = mofo_base

        def wregion():
            A.ptr = wbase

        def wavefront(stages, items):
            n, m = len(items), len(stages)
            for step in range(n + m - 1):
                for j in reversed(range(m)):
                    i = step - j
                    if 0 <= i < n:
                        stages[j](items[i])

        def rstd_ops(ssq, rs, P):
            S.add("dve", lambda e: e.tensor_scalar(out=rs[0:P, :], in0=ssq[0:P, :], scalar1=1.0 / D,
                                                   scalar2=EPS, op0=ALU.mult, op1=ALU.add),
                  reads=[ssq], writes=[rs])
            S.add("act", lambda e: e.activation(out=rs[0:P, :], in_=rs[0:P, :], func=AF.Sqrt),
                  reads=[rs], writes=[rs])
            S.add("dve", lambda e: e.reciprocal(out=rs[0:P, :], in_=rs[0:P, :]), reads=[rs], writes=[rs])

        def transpose_evac(xb, P, gT, dstT, tok0):
            for half in range(2):
                b = nb()
                pb = PSB(b)

                def tr(e, half=half, pb=pb):
                    ins = None
                    for cc in range(8):
                        c = half * 8 + cc
                        ins = e.transpose(out=pb[:, cc * 128: cc * 128 + P],
                                          in_=xb[0:P, c * 128:(c + 1) * 128], identity=ident[0:P, 0:P])
                    return ins
                S.add("pe", tr, reads=[xb, ident], writes=[PS(b)])
                for cc in range(8):
                    c = half * 8 + cc
                    src_ps = pb[:, cc * 128: cc * 128 + P]
                    dst = dstT[:, c, tok0:tok0 + P]
                    if half == 0:
                        S.add("act", lambda e, s=src_ps, d=dst, c=c: e.activation(
                            out=d, in_=s, func=AF.Copy, scale=gT[:, c:c + 1]),
                            reads=[PS(b), gT], writes=[dst])
                    else:
                        S.add("dve", lambda e, s=src_ps, d=dst, c=c: e.tensor_scalar(
                            out=d, in0=s, scalar1=gT[:, c:c + 1], scalar2=None, op0=ALU.mult),
                            reads=[PS(b), gT], writes=[dst])

        def stage_AB():
            wregion()
            NST = 3
            xst = [A.alloc(D * 4, F32) for _ in range(NST)]
            xnb = [A.alloc(D * 2, BF16) for _ in range(NST)]
            items = []
            for k in range(9):
                P = HALO if k == 0 else 128
                row0 = 0 if k == 0 else HALO + (k - 1) * 128
                items.append(dict(k=k, P=P, row0=row0, src=xst[k % NST][0:P, :], xb=xnb[k % NST],
                                  ssq=st(1), rs=st(1),
                                  dstT=hTh if k == 0 else hT, tok0=0 if k == 0 else (k - 1) * 128))

            def s_load(it):
                it["load"] = dma("sp", it["src"], x_ext[it["row0"]:it["row0"] + it["P"], :])

            def s_sq(it):
                P = it["P"]
                S.add("act", lambda e: e.activation(out=it["xb"][0:P, :], in_=it["src"], func=AF.Square,
                                                    accum_out=it["ssq"][0:P, :]),
                      reads=[it["src"]], writes=[it["xb"], it["ssq"]])

            def s_rstd(it):
                rstd_ops(it["ssq"], it["rs"], it["P"])

            def s_scale(it):
                P = it["P"]
                S.add("dve", lambda e: e.tensor_scalar(out=it["xb"][0:P, :], in0=it["src"],
                                                       scalar1=it["rs"][0:P, :], scalar2=None, op0=ALU.mult),
                      reads=[it["src"], it["rs"]], writes=[it["xb"]])

            def s_tr(it):
                transpose_evac(it["xb"], it["P"], g1T, it["dstT"], it["tok0"])

            wavefront([s_load, s_sq, s_rstd, s_scale, s_tr], items)
            if dbg and "hT" in dbg:
                dma("sp", dbg_d["hT"], hT)
                dma("sp", dbg_d["hTh"], hTh)
            if last_stage == "P0":
                return

            wregion()
            ug = [A.alloc(HT * 4, F32), A.alloc(HT * 4, F32)]
            tmpA = [A.alloc(HT * 4, F32), A.alloc(HT * 4, F32)]
            t16 = A.alloc(HALO * 4, F32)
            sgub = A.alloc(8 * 128 * 4, F32, "p (h i) -> p h i", h=8)
            vg = [A.alloc(DS * 4, F32), A.alloc(DS * 4, F32)]
            lng = A.alloc(DS * 4, F32)
            lnb = A.alloc(DS * 4, F32)
            L = HALO + TOK
            A.ptr = mergedT_base
            NRA = 4
            ringA = [A.alloc(16 * 128 * 2, BF16, "p (k n) -> p k n", k=16) for _ in range(NRA)]
            pT1 = A.alloc(L * 4, F32)
            pT = [pT1, pT1]
            sA = A.alloc(L * 4, F32)
            sB = A.alloc(L * 4, F32)
            assert A.ptr <= mergedT_base + 16 * TOK * 2
            dma("sp", sgub, sgub_d)
            ra = [0]

            def wtile(src, after=()):
                slot = ringA[ra[0] % NRA]
                ra[0] += 1
                dma("pool", slot, src, after=after)
                return slot

            for c in range(8):
                g = c // 2
                w = 2 << g
                slot = wtile(w_p[c], after=[items[6]["load"]] if c in (1, 2, 3) else ())
                if c == 3:
                    for ct in range(2):
                        dma("pool", Wv[:, ct], w_v[ct], after=[items[8]["load"]])
                pj = pT[c % 2]
                bh = nb()

                def mmph(e, slot=slot, bh=bh):
                    ins = None
                    for kc in range(16):
                        ins = e.matmul(PS(bh)[:, 0:HALO], lhsT=slot[:, kc, :], rhs=hTh[:, kc, :],
                                       start=(kc == 0), stop=(kc == 15))
                    return ins
                S.add("pe", mmph, reads=[slot, hTh], writes=[PS(bh)])
                S.add("dve", lambda e, bh=bh, pj=pj: e.tensor_copy(out=pj[:, 0:HALO], in_=PS(bh)[:, 0:HALO]),
                      reads=[PS(bh)], writes=[pj[:, 0:HALO]])
                for hf in range(2):
                    b1 = nb()

                    def mmp(e, slot=slot, b1=b1, hf=hf):
                        ins = None
                        for kc in range(16):
                            ins = e.matmul(PS(b1), lhsT=slot[:, kc, :], rhs=hT[:, kc, hf * HT:(hf + 1) * HT],
                                           start=(kc == 0), stop=(kc == 15))
                        return ins
                    S.add("pe", mmp, reads=[slot, hT[:, :, hf * HT:(hf + 1) * HT]], writes=[PS(b1)])
                    dstp = pj[:, HALO + hf * HT: HALO + (hf + 1) * HT]
                    S.add("act", lambda e, b1=b1, d=dstp: e.activation(out=d, in_=PS(b1), func=AF.Copy),
                          reads=[PS(b1)], writes=[dstp])
                cur = pj
                lo = 0
                step = 1
                bufs = [sA, sB]
                bi = 0
                while step < w:
                    nlo = lo + step
                    dst = bufs[bi]
                    S.add("dve", lambda e, dst=dst, cur=cur, nlo=nlo, step=step: e.tensor_tensor(
                        out=dst[:, nlo:L], in0=cur[:, nlo:L], in1=cur[:, nlo - step:L - step], op=ALU.add),
                        reads=[cur], writes=[dst])
                    cur = dst
                    lo = nlo
                    step *= 2
                    bi ^= 1
                S.add("dve", lambda e, cur=cur, pj=pj, c=c, w=w: e.scalar_tensor_tensor(
                    out=pooledT[:, c, :], in0=cur[:, HALO:], scalar=1.0 / w, in1=pj[:, HALO:],
                    op0=ALU.mult, op1=ALU.subtract),
                    reads=[cur, pj], writes=[pooledT[:, c, :]])
                S.add("dve", lambda e, cur=cur, g=g: e.tensor_tensor(
                    out=t16, in0=cur[:, HALO:2 * HALO], in1=invf[:, g, :], op=ALU.mult),
                    reads=[cur, invf], writes=[t16])
                S.add("dve", lambda e, pj=pj, c=c: e.tensor_tensor(
                    out=pooledT[:, c, 0:HALO], in0=t16, in1=pj[:, HALO:2 * HALO], op=ALU.subtract),
                    reads=[t16, pj], writes=[pooledT[:, c, 0:HALO]])
            if dbg and "pooledT" in dbg:
                dma("sp", dbg_d["pooledT"], pooledT)
            if last_stage == "A3":
                return

            dma("sp", lng, lng_d, group="ln")
            dma("sp", lnb, lnb_d, group="ln")
            for tb in range(8):
                vgj = vg[tb % 2]
                bst = st(12)
                mv = st(3)
                for ct in range(2):
                    b = nb()

                    def mmv(e, tb=tb, ct=ct, b=b):
                        ins = None
                        for kc in range(16):
                            ins = e.matmul(PS(b), lhsT=hT[:, kc, tb * 128:(tb + 1) * 128],
                                           rhs=Wv[:, ct, kc, :], start=(kc == 0), stop=(kc == 15))
                        return ins
                    S.add("pe", mmv, reads=[hT[:, :, tb * 128:(tb + 1) * 128], Wv[:, ct]], writes=[PS(b)])
                    dstv = vgj[:, ct * 512:(ct + 1) * 512]
                    S.add("act", lambda e, b=b, d=dstv: e.activation(out=d, in_=PS(b), func=AF.Gelu),
                          reads=[PS(b)], writes=[dstv])
                    S.add("dve", lambda e, d=dstv, o=bst[:, ct * 6:(ct + 1) * 6]: e.bn_stats(out=o, in_=d),
                          reads=[dstv], writes=[bst[:, ct * 6:(ct + 1) * 6]])
                S.add("dve", lambda e, bst=bst, mv=mv: e.bn_aggr(out=mv[:, 0:2], in_=bst),
                      reads=[bst], writes=[mv])
                S.add("dve", lambda e, mv=mv: e.tensor_scalar(out=mv[:, 2:3], in0=mv[:, 1:2], scalar1=EPS,
                                                              scalar2=None, op0=ALU.add),
                      reads=[mv], writes=[mv])
                S.add("act", lambda e, mv=mv: e.activation(out=mv[:, 2:3], in_=mv[:, 2:3], func=AF.Sqrt),
                      reads=[mv], writes=[mv])
                S.add("dve", lambda e, mv=mv: e.reciprocal(out=mv[:, 2:3], in_=mv[:, 2:3]),
                      reads=[mv], writes=[mv])
                S.add("dve", lambda e, v=vgj, mv=mv: e.tensor_scalar(out=v, in0=v, scalar1=mv[:, 0:1],
                                                                     scalar2=mv[:, 2:3], op0=ALU.subtract,
                                                                     op1=ALU.mult),
                      reads=[vgj, mv], writes=[vgj])
                S.add("dve", lambda e, v=vgj: e.tensor_tensor(out=v, in0=v, in1=lng, op=ALU.mult),
                      reads=[vgj, lng], writes=[vgj])
                S.add("dve", lambda e, v=vgj, tb=tb: e.tensor_tensor(out=v_ln[:, tb, :], in0=v, in1=lnb,
                                                                    op=ALU.add),
                      reads=[vgj, lnb], writes=[v_ln[:, tb, :]])
            if dbg and "v_ln" in dbg:
                dma("sp", dbg_d["v_ln"], v_ln)
            if last_stage == "A1":
                return

            for m in range(8):
                g = m // 2
                for hf in range(2):
                    b = nb()

                    def mmb(e, m=m, g=g, b=b, hf=hf):
                        ins = None
                        for kc in range(2):
                            ins = e.matmul(PS(b), lhsT=wpool[:, g, kc, (m % 2) * 128:(m % 2 + 1) * 128],
                                           rhs=pooledT[:, 2 * g + kc, hf * HT:(hf + 1) * HT],
                                           start=(kc == 0), stop=(kc == 1))
                        return ins
                    S.add("pe", mmb, reads=[wpool, pooledT[:, 2 * g:2 * g + 2, hf * HT:(hf + 1) * HT]],
                          writes=[PS(b)])
                    dsty = ybT[:, m, hf * HT:(hf + 1) * HT]
                    S.add("act", lambda e, m=m, b=b, d=dsty: e.activation(out=d, in_=PS(b), func=AF.Copy,
                                                                        scale=pscT[:, m:m + 1]),
                          reads=[PS(b), pscT], writes=[dsty])
            if dbg and "ybT" in dbg:
                dma("sp", dbg_d["ybT"], ybT)
            if last_stage == "A4":
                return

            for h in range(8):
                slot = wtile(w_u[h])
                for hf in range(2):
                    b1 = nb()

                    def mmu(e, slot=slot, b1=b1, hf=hf):
                        ins = None
                        for kc in range(16):
                            ins = e.matmul(PS(b1), lhsT=slot[:, kc, :], rhs=hT[:, kc, hf * HT:(hf + 1) * HT],
                                           start=(kc == 0), stop=(kc == 15))
                        return ins
                    S.add("pe", mmu, reads=[slot, hT[:, :, hf * HT:(hf + 1) * HT]], writes=[PS(b1)])
                    ugj = ug[hf]
                    S.add("act", lambda e, b1=b1, u=ugj: e.activation(out=u, in_=PS(b1), func=AF.Gelu),
                          reads=[PS(b1)], writes=[ugj])
                    b2 = nb()

                    def mmx(e, h=h, b2=b2, hf=hf):
                        ins = None
                        for t in range(4):
                            tb = hf * 4 + t
                            ins = e.matmul(PS(b2)[:, t * 128:(t + 1) * 128],
                                           lhsT=v_ln[:, tb, h * 128:(h + 1) * 128], rhs=sgwT[:, h, :],
                                           start=True, stop=True)
                        return ins
                    S.add("pe", mmx, reads=[v_ln[:, hf * 4:(hf + 1) * 4, h * 128:(h + 1) * 128], sgwT],
                          writes=[PS(b2)])
                    tj = tmpA[hf]
                    S.add("dve", lambda e, b2=b2, tj=tj, h=h: e.tensor_tensor(
                        out=tj.rearrange("p (t i) -> p t i", t=4),
                        in0=PS(b2).rearrange("p (t i) -> p t i", t=4),
                        in1=sgub[:, h:h + 1, :].to_broadcast([128, 4, 128]), op=ALU.add),
                        reads=[PS(b2), sgub], writes=[tj])
                    dsta = yAT[:, h, hf * HT:(hf + 1) * HT]
                    S.add("dve", lambda e, tj=tj, u=ugj, d=dsta: e.tensor_tensor(out=d, in0=tj, in1=u,
                                                                              op=ALU.mult),
                          reads=[tj, ugj], writes=[dsta])
            if dbg and "yAT" in dbg:
                dma("sp", dbg_d["yAT"], yAT)
            if last_stage == "A2":
                return

            wregion()
            ringB2 = A.alloc(48 * 128 * 2, BF16, "p (k n) -> p k n", k=48)
            A.ptr = mofo_scratch
            ringB = [A.alloc(48 * 128 * 2, BF16, "p (k n) -> p k n", k=48) for _ in range(2)] + [ringB2]
            sa = [A.alloc(HT * 4, F32), A.alloc(HT * 4, F32)]
            sb = [A.alloc(HT * 4, F32), A.alloc(HT * 4, F32)]
            assert A.ptr <= mofo_scratch + 4 * D * 4
            for d in range(16):
                slot = ringB[d % 3]
                dma("pool", slot, w_gab[d])
                for hf in range(2):
                    tk = slice(hf * HT, (hf + 1) * HT)
                    bs = [hf * 4 + i for i in range(4)]

                    def mmg(e, slot=slot, bs=bs, tk=tk):
                        ins = None
                        for kc in range(16):
                            ins = e.matmul(PS(bs[0]), lhsT=slot[:, kc, :], rhs=hT[:, kc, tk],
                                           start=(kc == 0), stop=(kc == 15))
                        for kc in range(16):
                            ins = e.matmul(PS(bs[1]), lhsT=slot[:, 16 + kc, :], rhs=hT[:, kc, tk],
                                           start=(kc == 0), stop=(kc == 15))
                        for kc in range(8):
                            ins = e.matmul(PS(bs[2]), lhsT=slot[:, 32 + kc, :], rhs=yAT[:, kc, tk],
                                           start=(kc == 0), stop=(kc == 7))
                        for kc in range(8):
                            ins = e.matmul(PS(bs[3]), lhsT=slot[:, 40 + kc, :], rhs=ybT[:, kc, tk],
                                           start=(kc == 0), stop=(kc == 7))
                        return ins
                    S.add("pe", mmg, reads=[slot, hT[:, :, tk], yAT[:, :, tk], ybT[:, :, tk]],
                          writes=[PS(b) for b in bs])
                    saj, sbj = sa[hf], sb[hf]
                    S.add("act", lambda e, b=bs[0], o=saj: e.activation(out=o, in_=PS(b), func=AF.Sigmoid),
                          reads=[PS(bs[0])], writes=[saj])
                    S.add("act", lambda e, b=bs[1], o=sbj: e.activation(out=o, in_=PS(b), func=AF.Sigmoid),
                          reads=[PS(bs[1])], writes=[sbj])
                    S.add("dve", lambda e, b=bs[2], o=saj: e.tensor_tensor(out=o, in0=o, in1=PS(b), op=ALU.mult),
                          reads=[saj, PS(bs[2])], writes=[saj])
                    S.add("dve", lambda e, b=bs[3], o=sbj: e.tensor_tensor(out=o, in0=o, in1=PS(b), op=ALU.mult),
                          reads=[sbj, PS(bs[3])], writes=[sbj])
                    dstm = mergedT[:, d, tk]
                    S.add("dve", lambda e, a=saj, bb=sbj, dd=dstm: e.tensor_tensor(out=dd, in0=a, in1=bb,
                                                                                 op=ALU.add),
                          reads=[saj, sbj], writes=[dstm])
            if dbg and "mergedT" in dbg:
                dma("sp", dbg_d["mergedT"], mergedT)

        def stage_CDE(hf):
            t0 = hf * HT
            tk0 = hf * HT
            wregion()
            NRD = 3
            ringD = [A.alloc(32 * 128 * 2, BF16, "p (k n) -> p k n", k=32) for _ in range(NRD)]
            sg = [A.alloc(HT * 4, F32), A.alloc(HT * 4, F32)]
            gp_off = A.ptr
            A.ptr = hid_base
            ringC_a = A.alloc(16 * 512 * 2, BF16, "p (k n) -> p k n", k=16)
            A.ptr = pooled_base
            ringC_b = A.alloc(16 * 512 * 2, BF16, "p (k n) -> p k n", k=16)
            ringC = [ringC_a, ringC_b]
            xnb = [A.alloc(D * 2, BF16) for _ in range(3)]
            assert A.ptr <= r1_end, (A.ptr, r1_end)
            A.ptr = wbase + 12 * 1024
            ringC0 = A.alloc(16 * 512 * 2, BF16, "p (k n) -> p k n", k=16)
            A.ptr = gp_off
            gp = A.alloc(D * 4, F32)
            c_parts = [st(4) for _ in range(4)]
            ringC3 = h2T.rearrange("p k n -> p (k n)").rearrange("p (k n) -> p k n", k=16)
            slots = [ringC0 if hf == 0 else ringC[0], ringC[1] if hf == 0 else mergedT[:, :, 0:HT], ringC[0], ringC3]
            dma("pool", slots[0], w_mix[0])
            for q_ in range(3):
                dma("pool", gp[:, q_ * 512:(q_ + 1) * 512], gp1_d[:, q_ * 512:(q_ + 1) * 512])
            op_et1 = dma("pool", slots[1], w_mix[1])
            for tb in range(4):
                dma("sp", xs[:, tb, :], x_ext[HALO + t0 + tb * 128: HALO + t0 + (tb + 1) * 128, :],
                    after=[op_et1])
            dma("pool", gp[:, 3 * 512:4 * 512], gp1_d[:, 3 * 512:4 * 512])
            dma("pool", slots[3], w_mix[3])

            def mm_ev(et, tb, defer=None):
                slot = slots[et]
                b = nb()

                def mmc(e):
                    ins = None
                    for kc in range(16):
                        ins = e.matmul(PS(b), lhsT=mergedT[:, kc, tk0 + tb * 128: tk0 + (tb + 1) * 128],
                                       rhs=slot[:, kc, :], start=(kc == 0), stop=(kc == 15))
                    return ins
                S.add("pe", mmc, reads=[slot, mergedT[:, :, tk0 + tb * 128: tk0 + (tb + 1) * 128]],
                      writes=[PS(b)])
                dst = mo[:, tb, et * 512:(et + 1) * 512]

                def ev():
                    junk = xnb[(et * 4 + tb) % 3][:, 0:512]
                    part = c_parts[tb][:, et:et + 1]
                    S.add("act", lambda e: e.activation(out=junk, in_=PS(b), func=AF.Square, accum_out=part),
                          reads=[PS(b)], writes=[junk, part])
                    g_ = gp[:, et * 512:(et + 1) * 512]
                    S.add("dve", lambda e: e.tensor_tensor(out=dst, in0=PS(b), in1=g_, op=ALU.mult),
                          reads=[PS(b), g_], writes=[dst])
                if defer is not None:
                    defer.append(ev)
                else:
                    ev()

            evacs = []
            for tb in range(4):
                mm_ev(0, tb, defer=evacs)
            for ev_ in evacs:
                ev_()
            dma("pool", slots[2], w_mix[2])
            for tb in range(4):
                mm_ev(1, tb)
            mm_ev(2, 0)
            mm_ev(2, 1)
            c_mm = {0: lambda: mm_ev(3, 0), 1: lambda: mm_ev(3, 1),
                    2: lambda: (mm_ev(2, 2), mm_ev(3, 2)), 3: lambda: (mm_ev(2, 3), mm_ev(3, 3))}
            items = [dict(tb=tb, xb=xnb[tb % 3], ssq=st(1), rs=st(1), ssq2=st(1), rs2=st(1)) for tb in range(4)]

            def c_sq(it, parts=c_parts):
                p_ = parts[it["tb"]]
                S.add("dve", lambda e: e.tensor_reduce(out=it["ssq"], in_=p_, axis=mybir.AxisListType.X, op=ALU.add),
                      reads=[p_], writes=[it["ssq"]])

            def c_rstd(it):
                rstd_ops(it["ssq"], it["rs"], 128)

            def c_res(it, src=mo):
                tb = it["tb"]
                s_ = src[:, tb, :]
                x_ = xs[:, tb, :]
                S.add("dve", lambda e: e.scalar_tensor_tensor(out=x_, in0=s_, scalar=it["rs"], in1=x_,
                                                              op0=ALU.mult, op1=ALU.add),
                      reads=[s_, it["rs"], x_], writes=[x_])

            def c_sq2(it):
                s_ = xs[:, it["tb"], :]
                S.add("act", lambda e: e.activation(out=it["xb"], in_=s_, func=AF.Square, accum_out=it["ssq2"]),
                      reads=[s_], writes=[it["xb"], it["ssq2"]])

            def c_rstd2(it):
                rstd_ops(it["ssq2"], it["rs2"], 128)

            def c_scale(it):
                s_ = xs[:, it["tb"], :]
                S.add("dve", lambda e: e.tensor_scalar(out=it["xb"], in0=s_, scalar1=it["rs2"], scalar2=None,
                                                       op0=ALU.mult),
                      reads=[s_, it["rs2"]], writes=[it["xb"]])

            def c_tr(it):
                transpose_evac(it["xb"], 128, g2T, h2T, it["tb"] * 128)

            def d_sub(f, lo, hi):
                slot = ringD[f % NRD]
                b1, b2 = nb(), nb()

                def mmd(e):
                    ins = None
                    for kc in range(16):
                        ins = e.matmul(PS(b1)[:, lo:hi], lhsT=slot[:, kc, :], rhs=h2T[:, kc, lo:hi],
                                       start=(kc == 0), stop=(kc == 15))
                    for kc in range(16):
                        ins = e.matmul(PS(b2)[:, lo:hi], lhsT=slot[:, 16 + kc, :], rhs=h2T[:, kc, lo:hi],
                                       start=(kc == 0), stop=(kc == 15))
                    return ins
                S.add("pe", mmd, reads=[slot, h2T[:, :, lo:hi]], writes=[PS(b1), PS(b2)])
                sgj = sg[f % 2]
                S.add("act", lambda e: e.activation(out=sgj[:, lo:hi], in_=PS(b1)[:, lo:hi], func=AF.Silu),
                      reads=[PS(b1)], writes=[sgj[:, lo:hi]])
                S.add("dve", lambda e: e.tensor_tensor(out=hidT[:, f, lo:hi], in0=sgj[:, lo:hi],
                                                       in1=PS(b2)[:, lo:hi], op=ALU.mult),
                      reads=[sgj[:, lo:hi], PS(b2)], writes=[hidT[:, f, lo:hi]])

            NSUB = NRD if last_stage not in ("C",) else 0
            for f in range(NSUB):
                dma("pool", ringD[f % NRD], w_gu[f])

            def c_dA(it):
                if it["tb"] == 1:
                    for f in range(NSUB):
                        d_sub(f, 0, 256)

            wavefront([lambda it: c_mm[it["tb"]](), c_sq, c_rstd, c_res, c_sq2, c_rstd2, c_scale, c_tr, c_dA],
                      items)
            for f in range(NSUB):
                d_sub(f, 256, 512)
            if dbg and "x1" in dbg and hf == dbg.get("_hf", 0):
                dma("sp", dbg_d["x1"].rearrange("(t p) d -> p t d", p=128), xs)
                dma("sp", dbg_d["h2T"], h2T)
            if last_stage == "C":
                return

            for f in range(NSUB, NFC):
                slot = ringD[f % NRD]
                dma("pool", slot, w_gu[f])
                b1, b2 = nb(), nb()

                def mmd(e, slot=slot, b1=b1, b2=b2):
                    ins = None
                    for kc in range(16):
                        ins = e.matmul(PS(b1), lhsT=slot[:, kc, :], rhs=h2T[:, kc, :],
                                       start=(kc == 0), stop=(kc == 15))
                    for kc in range(16):
                        ins = e.matmul(PS(b2), lhsT=slot[:, 16 + kc, :], rhs=h2T[:, kc, :],
                                       start=(kc == 0), stop=(kc == 15))
                    return ins
                S.add("pe", mmd, reads=[slot, h2T], writes=[PS(b1), PS(b2)])
                sgj = sg[f % 2]
                S.add("act", lambda e, b1=b1, o=sgj: e.activation(out=o, in_=PS(b1), func=AF.Silu),
                      reads=[PS(b1)], writes=[sgj])
                S.add("dve", lambda e, b2=b2, o=sgj, f=f: e.tensor_tensor(out=hidT[:, f, :], in0=o, in1=PS(b2),
                                                                         op=ALU.mult),
                      reads=[sgj, PS(b2)], writes=[hidT[:, f, :]])
            if last_stage == "D":
                return

            wregion()
            NRE = 7
            ringE = [A.alloc(4 * 512 * 2, BF16, "p (k n) -> p k n", k=4) for _ in range(NRE)]
            assert A.ptr <= gp_off
            A.ptr = gp_off
            gp2 = A.alloc(D * 4, F32)
            xne = [A.alloc(512 * 2, BF16) for _ in range(2)]
            assert A.ptr <= wbase + wsize
            dma("pool", gp2, gp2_d)
            e_parts = [st(4) for _ in range(4)]
            re_ = 0
            for et in range(4):
                banks = [(et % 2) * 4 + tb for tb in range(4)]
                for fg in range(11):
                    slot = ringE[re_ % NRE]
                    re_ += 1
                    dma("pool", slot, w_dn[et, fg])

                    def mme(e, slot=slot, fg=fg, banks=banks):
                        ins = None
                        for fl in range(4):
                            fc = fg * 4 + fl
                            for tb in range(4):
                                ins = e.matmul(PS(banks[tb]), lhsT=hidT[:, fc, tb * 128:(tb + 1) * 128],
                                               rhs=slot[:, fl, :], start=(fc == 0), stop=(fc == NFC - 1))
                        return ins
                    S.add("pe", mme, reads=[slot, hidT[:, fg * 4:(fg + 1) * 4, :]], writes=[PS(b) for b in banks])
                for tb in range(4):
                    b = banks[tb]
                    dst = fo[:, tb, et * 512:(et + 1) * 512]
                    junk = xne[tb % 2]
                    part = e_parts[tb][:, et:et + 1]
                    S.add("act", lambda e, b=b, j=junk, p=part: e.activation(out=j, in_=PS(b), func=AF.Square,
                                                                             accum_out=p),
                          reads=[PS(b)], writes=[junk, part])
                    g_ = gp2[:, et * 512:(et + 1) * 512]
                    S.add("dve", lambda e, b=b, d=dst, g_=g_: e.tensor_tensor(out=d, in0=PS(b), in1=g_, op=ALU.mult),
                          reads=[PS(b), g_], writes=[dst])
            items = [dict(tb=tb, xb=xne[tb % 2], ssq=st(1), rs=st(1)) for tb in range(4)]

            def e_out(it):
                tb = it["tb"]
                o_ = out_d[t0 + tb * 128: t0 + (tb + 1) * 128, :]
                S.add("sp", lambda e: e.dma_start(out=o_, in_=xs[:, tb, :]),
                      reads=[xs[:, tb, :]], writes=[("dram", "out", tb)], dma=True)

            wavefront([lambda it: c_sq(it, parts=e_parts), c_rstd], items)
            wavefront([lambda it: c_res(it, src=fo), e_out], items)

        stage_AB()
        if last_stage in (None, "C", "D", "E"):
            for hf in range(nhalves):
                stat_ctr[0] = 128
                stage_CDE(hf)

        fin = S.add("sp", None, reads=[out_d] + list(dbg_d.values()), writes=[])

        S.analyze()
        outnames = {("dram", "out")} | {("dram", "dbg_" + n) for n in dbg_d}
        for i, op in enumerate(S.ops):
            if op.dma and any((w in outnames or w[:2] == ("dram", "out")) for w in op.W) and i not in fin.deps:
                fin.deps.append(i)

        def mksem(name):
            h = es.enter_context(nc.semaphore("s" + name))
            sems.append(h)
            return h

        run = S.emit(mksem)
        with nc.Block() as block:
            @block.sync
            def _(e):
                run("sp", e)

            @block.gpsimd
            def _(e):
                run("pool", e)

            @block.tensor
            def _(e):
                run("pe", e)

            @block.scalar
            def _(e):
                run("act", e)

            @block.vector
            def _(e):
                run("dve", e)
    return nc, S


def _tile_cols(W, ncol):
    K, N = W.shape
    return np.ascontiguousarray(W.reshape(K // 128, 128, N // ncol, ncol).transpose(2, 1, 0, 3))


def prepare_inputs(x, norm1_pre, w_in, v_ln_g, v_ln_b, sgu_w, sgu_b, pool_w, pool_scale,
                   w_a_out, w_b_out, w_mix_out, norm1_post, norm2_pre, w_ffn_gate,
                   w_ffn_up, w_ffn_down, norm2_post):
    f32 = np.float32
    x = np.asarray(x, f32)
    W_in = np.asarray(w_in, f32)[0]
    shared = {}
    shared["w_u"] = _tile_cols(W_in[:, 0:1024], 128)
    shared["w_v"] = _tile_cols(W_in[:, 1024:2048], 512)
    shared["w_p"] = _tile_cols(W_in[:, 2048:3072], 128)
    ga = _tile_cols(W_in[:, 3072:5120], 128)
    gb = _tile_cols(W_in[:, 5120:7168], 128)
    wa = _tile_cols(np.asarray(w_a_out, f32)[0], 128)
    wb = _tile_cols(np.asarray(w_b_out, f32)[0], 128)
    shared["w_gab"] = np.ascontiguousarray(np.concatenate([ga, gb, wa, wb], axis=2))
    pw = np.asarray(pool_w, f32)[0]
    shared["w_pool"] = np.ascontiguousarray(pw.reshape(4, 2, 128, 256).transpose(2, 0, 1, 3))
    shared["w_mix"] = _tile_cols(np.asarray(w_mix_out, f32)[0], 512)
    wg = _tile_cols(np.asarray(w_ffn_gate, f32)[0], 128)
    wu = _tile_cols(np.asarray(w_ffn_up, f32)[0], 128)
    shared["w_gu"] = np.ascontiguousarray(np.concatenate([wg, wu], axis=2))
    wd = np.asarray(w_ffn_down, f32)[0]
    shared["w_dn"] = np.ascontiguousarray(wd.reshape(11, 4, 128, 4, 512).transpose(3, 0, 2, 1, 4))
    sw = np.asarray(sgu_w, f32)[0]
    shared["sgwT"] = np.ascontiguousarray(sw.transpose(2, 0, 1))
    shared["sgub"] = np.ascontiguousarray(np.broadcast_to(np.asarray(sgu_b, f32)[0][None], (128, 8, 128)))
    shared["lng"] = np.ascontiguousarray(np.broadcast_to(np.asarray(v_ln_g, f32)[0][None], (128, DS)))
    shared["lnb"] = np.ascontiguousarray(np.broadcast_to(np.asarray(v_ln_b, f32)[0][None], (128, DS)))
    shared["gp1"] = np.ascontiguousarray(np.broadcast_to(np.asarray(norm1_post, f32)[0][None], (128, D)))
    shared["gp2"] = np.ascontiguousarray(np.broadcast_to(np.asarray(norm2_post, f32)[0][None], (128, D)))
    shared["g1T"] = np.ascontiguousarray(np.asarray(norm1_pre, f32)[0].reshape(16, 128).T)
    shared["g2T"] = np.ascontiguousarray(np.asarray(norm2_pre, f32)[0].reshape(16, 128).T)
    shared["pscT"] = np.ascontiguousarray(np.asarray(pool_scale, f32)[0].reshape(8, 128).T)
    shared["ident"] = np.eye(128, dtype=f32).astype(ml_dtypes.bfloat16)
    inv_std = np.zeros((4, 16), f32)
    inv_first = np.zeros((4, 16), f32)
    for g in range(4):
        w = 2 << g
        inv_std[g, :] = 1.0 / w
        inv_first[g, :] = 1.0 / np.minimum(np.arange(1, 17), w)
    in_maps = []
    S = x.shape[1]
    cores_per_b = S // TOK
    for c in range(NCORES):
        b, q = divmod(c, cores_per_b)
        t0 = q * TOK
        xe = np.zeros((HALO + TOK, D), f32)
        xe[HALO:] = x[b, t0:t0 + TOK]
        if q > 0:
            xe[:HALO] = x[b, t0 - HALO:t0]
        m = dict(shared)
        m["x_ext"] = xe
        iv = inv_first if q == 0 else inv_std
        m["invf"] = np.ascontiguousarray(np.broadcast_to(iv[None], (128, 4, 16)))
        in_maps.append(m)
    return in_maps


_CACHE = {}


def kernel(**inputs):
    in_maps = prepare_inputs(**inputs)
    if "nc" not in _CACHE:
        _CACHE["nc"] = build_program()[0]
    nc = _CACHE["nc"]
    res = run_bass_kernel_spmd(nc, in_maps, core_ids=list(range(NCORES)))
    x = inputs["x"]
    B, S, _ = x.shape
    out = np.empty((B, S, D), np.float32)
    cores_per_b = S // TOK
    for c in range(NCORES):
        b, q = divmod(c, cores_per_b)
        out[b, q * TOK:(q + 1) * TOK] = res.results[c]["out"]
    return out
```

```python
import numpy as np
import ml_dtypes
import concourse.bass as bass
import concourse.mybir as mybir
from concourse.bass_utils import run_bass_kernel_spmd

F32 = mybir.dt.float32
BF16 = mybir.dt.bfloat16
U8 = mybir.dt.uint8
AF = mybir.ActivationFunctionType
ALU = mybir.AluOpType
ESZ = {F32: 4, BF16: 2, U8: 1}

D = 2048
DS = 1024
DFF = 5632
NFC = DFF // 128
EPS = 1e-6
NCORES = 8
TOK = 1024
HT = 512
HALO = 16
ARENA_BYTES = 212736
GRAN = 256


class Op:
    __slots__ = ("eng", "fn", "R", "W", "dma", "group", "deps", "signal", "sem", "sigval", "key", "after", "idx")

    def __init__(self, eng, fn, R, W, dma, group):
        self.eng, self.fn, self.R, self.W, self.dma, self.group = eng, fn, R, W, dma, group
        self.deps = []
        self.signal = False
        self.sem = None
        self.sigval = 0
        self.key = None
        self.after = []
        self.idx = -1


class Sched:
    def __init__(self, nc):
        self.nc = nc
        self.ops = []

    def res(self, x):
        if not hasattr(x, "tensor"):
            return [x]
        name = x.tensor.name
        es = ESZ[x.dtype]
        if name == "arena":
            g, sp = GRAN, "sb"
            fine = getattr(self, "fine", None)
        elif name == "psum":
            g, sp = 2048, "ps"
        else:
            return [("dram", name)]
        pstride = x.ap[0][0]
        off = x.offset % pstride if pstride else x.offset
        dims = [(st, cnt) for st, cnt in x.ap[1:] if cnt > 1]
        span = 1
        for st, cnt in dims:
            span += (cnt - 1) * abs(st)
        lo = off * es
        hi = (off + span) * es
        if sp == "sb" and fine and lo >= fine[0] and hi <= fine[1]:
            return [("st", i) for i in range(lo // 4, (hi - 1) // 4 + 1)]
        if dims and dims[-1][0] == 1:
            run = dims[-1][1]
            outer = dims[:-1]
        else:
            run, outer = 1, dims
        nruns = 1
        for st, cnt in outer:
            nruns *= cnt
        if outer and nruns <= 512 and all(st >= 0 for st, _ in outer):
            out = set()
            offs = [0]
            for st, cnt in outer:
                offs = [o + st * i for o in offs for i in range(cnt)]
            for o in offs:
                a = (off + o) * es
                b = (off + o + run) * es
                out.update(range(a // g, (b - 1) // g + 1))
            return [(sp, i) for i in sorted(out)]
        return [(sp, i) for i in range(lo // g, (hi - 1) // g + 1)]

    def add(self, eng, fn, reads=(), writes=(), dma=False, group=None, after=()):
        R, W = [], []
        for x in reads:
            R.extend(self.res(x))
        for x in writes:
            W.extend(self.res(x))
        op = Op(eng, fn, R, W, dma, group)
        if dma:
            op.key = group if group is not None else ("k", eng) + tuple(W[0])
        op.after = list(after)
        op.idx = len(self.ops)
        self.ops.append(op)
        return op

    def analyze(self):
        last_w, readers = {}, {}
        for i, op in enumerate(self.ops):
            deps = {}
            for r in op.R:
                w = last_w.get(r)
                if w is not None:
                    deps.setdefault(w, set()).add("RAW")
                if r[0] == "ps":
                    for rd in readers.get(r, ()):
                        if self.ops[rd].eng != op.eng:
                            deps.setdefault(rd, set()).add("PSRAR")
            for r in op.W:
                w = last_w.get(r)
                if w is not None:
                    deps.setdefault(w, set()).add("WAW")
                for rd in readers.get(r, ()):
                    deps.setdefault(rd, set()).add("WAR")
            for r in op.W:
                last_w[r] = i
                readers[r] = []
            for r in op.R:
                lst = readers.setdefault(r, [])
                if not lst or lst[-1] != i:
                    lst.append(i)
            for a in op.after:
                deps.setdefault(a.idx, set()).add("RAW")
            for d, kinds in deps.items():
                if d == i:
                    continue
                dop = self.ops[d]
                if dop.dma and op.dma and dop.group is not None and dop.group == op.group:
                    need = False
                elif dop.dma or op.dma:
                    need = True
                elif dop.eng == op.eng:
                    need = op.eng != "pe" and bool(kinds - {"PSRAR"})
                else:
                    need = True
                if need:
                    op.deps.append(d)
                    dop.signal = True

    def emit(self, sems_ctx):
        nc = self.nc
        cnt = {}
        group_total = {}
        for op in self.ops:
            if op.dma:
                op.signal = True
                k = ("dma", op.key)
                cnt[k] = cnt.get(k, 0) + 16
                op.sigval = cnt[k]
                op.sem = k
                if op.group is not None:
                    group_total[k] = cnt[k]
            elif op.signal:
                k = ("eng", op.eng)
                cnt[k] = cnt.get(k, 0) + 1
                op.sigval = cnt[k]
                op.sem = k
        for op in self.ops:
            if op.dma and op.group is not None:
                op.sigval = group_total[op.sem]
        semh = {}
        for k in cnt:
            semh[k] = sems_ctx(str(len(semh)))
        self.nsems = len(semh)
        per_eng = {}
        for op in self.ops:
            per_eng.setdefault(op.eng, []).append(op)

        def run(engname, e):
            waited = {}
            for op in per_eng.get(engname, []):
                need = {}
                for d in op.deps:
                    dop = self.ops[d]
                    if need.get(dop.sem, 0) < dop.sigval:
                        need[dop.sem] = dop.sigval
                for s, v in need.items():
                    if waited.get(s, 0) < v:
                        e.wait_ge(semh[s], v)
                        waited[s] = v
                if op.fn is None:
                    continue
                ins = op.fn(e)
                if op.signal:
                    ins.then_inc(semh[op.sem], 16 if op.dma else 1)
        return run


class Arena:
    def __init__(self, ap):
        self.ap = ap
        self.ptr = 0

    def alloc(self, nbytes, dtype, pattern=None, **kw):
        off = (self.ptr + GRAN - 1) // GRAN * GRAN if nbytes >= GRAN else (self.ptr + 31) // 32 * 32
        self.ptr = off + nbytes
        assert self.ptr <= ARENA_BYTES, f"arena overflow {self.ptr}"
        v = self.ap[:, off:off + nbytes].bitcast(dtype)
        if pattern:
            v = v.rearrange(pattern, **kw)
        return v


def build_program(dbg=None, nhalves=2, last_stage=None):
    nc = bass.Bass("TRN2", target_bir_lowering=False)
    dram = {}

    def din(name, shape, dt=F32):
        dram[name] = nc.dram_tensor(name, list(shape), dt, kind="ExternalInput").ap()
        return dram[name]

    x_ext = din("x_ext", [HALO + TOK, D])
    w_v = din("w_v", [2, 128, 16, 512])
    w_u = din("w_u", [8, 128, 16, 128])
    w_p = din("w_p", [8, 128, 16, 128])
    w_gab = din("w_gab", [16, 128, 48, 128])
    w_pool = din("w_pool", [128, 4, 2, 256])
    w_mix = din("w_mix", [4, 128, 16, 512])
    w_gu = din("w_gu", [NFC, 128, 32, 128])
    w_dn = din("w_dn", [4, 11, 128, 4, 512])
    sgwT_d = din("sgwT", [128, 8, 128])
    sgub_d = din("sgub", [128, 8, 128])
    lng_d = din("lng", [128, DS])
    lnb_d = din("lnb", [128, DS])
    gp1_d = din("gp1", [128, D])
    gp2_d = din("gp2", [128, D])
    g1T_d = din("g1T", [128, 16])
    g2T_d = din("g2T", [128, 16])
    pscT_d = din("pscT", [128, 8])
    invf_d = din("invf", [128, 4, 16])
    ident_d = din("ident", [128, 128], BF16)
    out_d = nc.dram_tensor("out", [TOK, D], F32, kind="ExternalOutput").ap()
    dbg_d = {}
    if dbg:
        for name, (shape, dt) in dbg.items():
            dbg_d[name] = nc.dram_tensor("dbg_" + name, list(shape), dt, kind="ExternalOutput").ap()

    sems = []
    import contextlib
    with contextlib.ExitStack() as es:
        arena_t = es.enter_context(nc.sbuf_tensor("arena", [128, ARENA_BYTES], U8))
        psum_t = es.enter_context(nc.psum_tensor("psum", [128, 8, 512], F32))
        A = Arena(arena_t)
        S = Sched(nc)

        def PS(b):
            return psum_t[:, b, :]

        def PSB(b):
            return psum_t[:, b, :].bitcast(BF16)

        bank_ctr = [0]

        def nb():
            b = bank_ctr[0]
            bank_ctr[0] = (b + 1) % 8
            return b

        ident = A.alloc(256, BF16)
        g1T = A.alloc(64, F32)
        g2T = A.alloc(64, F32)
        pscT = A.alloc(32, F32)
        invf = A.alloc(256, F32, "p (g t) -> p g t", g=4)
        sgwT = A.alloc(2048, BF16, "p (h i) -> p h i", h=8)
        wpool = A.alloc(4096, BF16, "p (g k n) -> p g k n", g=4, k=2)
        A.ptr = (A.ptr + GRAN - 1) // GRAN * GRAN
        S.fine = (A.ptr, A.ptr + 1024)
        stat = A.alloc(4 * 256, F32)
        A.ptr = (A.ptr + GRAN - 1) // GRAN * GRAN
        stat_ctr = [0]

        def st(n):
            o = stat_ctr[0]
            stat_ctr[0] = o + n
            assert stat_ctr[0] <= 256
            return stat[:, o:o + n]

        def dma(eng, out, in_, group=None, after=()):
            return S.add(eng, lambda e, o=out, i=in_: e.dma_start(out=o, in_=i),
                         reads=[in_], writes=[out], dma=True, group=group, after=after)

        def emit_consts(first_x_load):
            dma("sp", ident, ident_d, group="c0")
            dma("sp", g1T, g1T_d, group="c0")
            dma("sp", g2T, g2T_d, group="c0")
            dma("sp", pscT, pscT_d, group="c0")
            dma("sp", invf, invf_d, group="c0")
            dma("pool", sgwT, sgwT_d, after=[first_x_load])
            dma("pool", wpool, w_pool)
            S.add("dve", lambda e: e.memset(sgwT[64:128, :, 0:64], 0.0), reads=[], writes=[sgwT])

        A.ptr = (A.ptr + GRAN - 1) // GRAN * GRAN
        mergedT_base = A.ptr
        mergedT = A.alloc(16 * TOK * 2, BF16, "p (k n) -> p k n", k=16)
        r1 = A.ptr
        hT = A.alloc(16 * TOK * 2, BF16, "p (k n) -> p k n", k=16)
        hTh = A.alloc(16 * HALO * 2, BF16, "p (k n) -> p k n", k=16)
        v_ln = A.alloc(8 * DS * 2, BF16, "p (t d) -> p t d", t=8)
        yAT = A.alloc(8 * TOK * 2, BF16, "p (k n) -> p k n", k=8)
        pooled_base = (A.ptr + GRAN - 1) // GRAN * GRAN
        pooledT = A.alloc(8 * TOK * 2, BF16, "p (k n) -> p k n", k=8)
        ybT = A.alloc(8 * TOK * 2, BF16, "p (k n) -> p k n", k=8)
        r1_end = A.ptr
        A.ptr = r1
        xs = A.alloc(4 * D * 4, F32, "p (t d) -> p t d", t=4)
        h2T = A.alloc(16 * HT * 2, BF16, "p (k n) -> p k n", k=16)
        hid_base = A.ptr
        hidT = A.alloc(NFC * HT * 2, BF16, "p (k n) -> p k n", k=NFC)
        A.ptr = max(A.ptr, r1_end)
        mofo_base = A.ptr
        mo = A.alloc(4 * D * 4, F32, "p (t d) -> p t d", t=4)
        fo = mo
        A.ptr = mofo_base
        Wv = A.alloc(2 * 16 * 512 * 2, BF16, "p (c k n) -> p c k n", c=2, k=16)
        wbase = A.ptr
        wsize = ARENA_BYTES - wbase
        assert wsize >= 38 * 1024, wsize
        mofo_scratch = mofo_base

        def wregion():
            A.ptr = wbase

        def wavefront(stages, items):
            n, m = len(items), len(stages)
            for step in range(n + m - 1):
                for j in reversed(range(m)):
                    i = step - j
                    if 0 <= i < n:
                        stages[j](items[i])

        def rstd_ops(ssq, rs, P):
            S.add("dve", lambda e: e.tensor_scalar(out=rs[0:P, :], in0=ssq[0:P, :], scalar1=1.0 / D,
                                                   scalar2=EPS, op0=ALU.mult, op1=ALU.add),
                  reads=[ssq], writes=[rs])
            S.add("act", lambda e: e.activation(out=rs[0:P, :], in_=rs[0:P, :], func=AF.Sqrt),
                  reads=[rs], writes=[rs])
            S.add("dve", lambda e: e.reciprocal(out=rs[0:P, :], in_=rs[0:P, :]), reads=[rs], writes=[rs])

        def transpose_evac(xb, P, gT, dstT, tok0):
            for half in range(2):
                b = nb()
                pb = PSB(b)

                def tr(e, half=half, pb=pb):
                    ins = None
                    for cc in range(8):
                        c = half * 8 + cc
                        ins = e.transpose(out=pb[:, cc * 128: cc * 128 + P],
                                          in_=xb[0:P, c * 128:(c + 1) * 128], identity=ident[0:P, 0:P])
                    return ins
                S.add("pe", tr, reads=[xb, ident], writes=[PS(b)])
                for cc in range(8):
                    c = half * 8 + cc
                    src_ps = pb[:, cc * 128: cc * 128 + P]
                    dst = dstT[:, c, tok0:tok0 + P]
                    if half == 0:
                        S.add("act", lambda e, s=src_ps, d=dst, c=c: e.activation(
                            out=d, in_=s, func=AF.Copy, scale=gT[:, c:c + 1]),
                            reads=[PS(b), gT], writes=[dst])
                    else:
                        S.add("dve", lambda e, s=src_ps, d=dst, c=c: e.tensor_scalar(
                            out=d, in0=s, scalar1=gT[:, c:c + 1], scalar2=None, op0=ALU.mult),
                            reads=[PS(b), gT], writes=[dst])

        def stage_AB():
            wregion()
            NST = 3
            xst = [A.alloc(D * 4, F32) for _ in range(NST)]
            xnb = [A.alloc(D * 2, BF16) for _ in range(NST)]
            items = []
            for k in range(9):
                P = HALO if k == 0 else 128
                row0 = 0 if k == 0 else HALO + (k - 1) * 128
                items.append(dict(k=k, P=P, row0=row0, src=xst[k % NST][0:P, :], xb=xnb[k % NST],
                                  ssq=st(1), rs=st(1),
                                  dstT=hTh if k == 0 else hT, tok0=0 if k == 0 else (k - 1) * 128))

            def s_load(it):
                it["load"] = dma("sp", it["src"], x_ext[it["row0"]:it["row0"] + it["P"], :])
                if it["k"] == 1:
                    emit_consts(it["load"])

            def s_sq(it):
                P = it["P"]
                S.add("act", lambda e: e.activation(out=it["xb"][0:P, :], in_=it["src"], func=AF.Square,
                                                    accum_out=it["ssq"][0:P, :]),
                      reads=[it["src"]], writes=[it["xb"], it["ssq"]])

            def s_rstd(it):
                rstd_ops(it["ssq"], it["rs"], it["P"])

            def s_scale(it):
                P = it["P"]
                S.add("dve", lambda e: e.tensor_scalar(out=it["xb"][0:P, :], in0=it["src"],
                                                       scalar1=it["rs"][0:P, :], scalar2=None, op0=ALU.mult),
                      reads=[it["src"], it["rs"]], writes=[it["xb"]])

            def s_tr(it):
                transpose_evac(it["xb"], it["P"], g1T, it["dstT"], it["tok0"])

            wavefront([s_load, s_sq, s_rstd, s_scale, s_tr], items)
            if dbg and "hT" in dbg:
                dma("sp", dbg_d["hT"], hT)
                dma("sp", dbg_d["hTh"], hTh)
            if last_stage == "P0":
                return

            wregion()
            ug = [A.alloc(HT * 4, F32), A.alloc(HT * 4, F32)]
            tmpA = [A.alloc(HT * 4, F32), A.alloc(HT * 4, F32)]
            t16 = A.alloc(HALO * 4, F32)
            sgub = A.alloc(8 * 128 * 4, F32, "p (h i) -> p h i", h=8)
            vg = [A.alloc(DS * 4, F32), A.alloc(DS * 4, F32)]
            lng = A.alloc(DS * 4, F32)
            lnb = A.alloc(DS * 4, F32)
            L = HALO + TOK
            A.ptr = mergedT_base
            NRA = 4
            ringA = [A.alloc(16 * 128 * 2, BF16, "p (k n) -> p k n", k=16) for _ in range(NRA)]
            pT1 = A.alloc(L * 4, F32)
            pT = [pT1, pT1]
            sA = A.alloc(L * 4, F32)
            sB = A.alloc(L * 4, F32)
            assert A.ptr <= mergedT_base + 16 * TOK * 2
            dma("sp", sgub, sgub_d)
            ra = [0]

            def wtile(src, after=()):
                slot = ringA[ra[0] % NRA]
                ra[0] += 1
                dma("pool", slot, src, after=after)
                return slot

            for c in range(8):
                g = c // 2
                w = 2 << g
                slot = wtile(w_p[c], after=[items[6]["load"]] if c in (1, 2, 3) else ())
                if c == 3:
                    for ct in range(2):
                        dma("pool", Wv[:, ct], w_v[ct], after=[items[8]["load"]])
                pj = pT[c % 2]
                bh = nb()

                def mmph(e, slot=slot, bh=bh):
                    ins = None
                    for kc in range(16):
                        ins = e.matmul(PS(bh)[:, 0:HALO], lhsT=slot[:, kc, :], rhs=hTh[:, kc, :],
                                       start=(kc == 0), stop=(kc == 15))
                    return ins
                S.add("pe", mmph, reads=[slot, hTh], writes=[PS(bh)])
                S.add("dve", lambda e, bh=bh, pj=pj: e.tensor_copy(out=pj[:, 0:HALO], in_=PS(bh)[:, 0:HALO]),
                      reads=[PS(bh)], writes=[pj[:, 0:HALO]])
                for hf in range(2):
                    b1 = nb()

                    def mmp(e, slot=slot, b1=b1, hf=hf):
                        ins = None
                        for kc in range(16):
                            ins = e.matmul(PS(b1), lhsT=slot[:, kc, :], rhs=hT[:, kc, hf * HT:(hf + 1) * HT],
                                           start=(kc == 0), stop=(kc == 15))
                        return ins
                    S.add("pe", mmp, reads=[slot, hT[:, :, hf * HT:(hf + 1) * HT]], writes=[PS(b1)])
                    dstp = pj[:, HALO + hf * HT: HALO + (hf + 1) * HT]
                    S.add("act", lambda e, b1=b1, d=dstp: e.activation(out=d, in_=PS(b1), func=AF.Copy),
                          reads=[PS(b1)], writes=[dstp])
                cur = pj
                lo = 0
                step = 1
                bufs = [sA, sB]
                bi = 0
                while step < w:
                    nlo = lo + step
                    dst = bufs[bi]
                    S.add("dve", lambda e, dst=dst, cur=cur, nlo=nlo, step=step: e.tensor_tensor(
                        out=dst[:, nlo:L], in0=cur[:, nlo:L], in1=cur[:, nlo - step:L - step], op=ALU.add),
                        reads=[cur], writes=[dst])
                    cur = dst
                    lo = nlo
                    step *= 2
                    bi ^= 1
                S.add("dve", lambda e, cur=cur, pj=pj, c=c, w=w: e.scalar_tensor_tensor(
                    out=pooledT[:, c, :], in0=cur[:, HALO:], scalar=1.0 / w, in1=pj[:, HALO:],
                    op0=ALU.mult, op1=ALU.subtract),
                    reads=[cur, pj], writes=[pooledT[:, c, :]])
                S.add("dve", lambda e, cur=cur, g=g: e.tensor_tensor(
                    out=t16, in0=cur[:, HALO:2 * HALO], in1=invf[:, g, :], op=ALU.mult),
                    reads=[cur, invf], writes=[t16])
                S.add("dve", lambda e, pj=pj, c=c: e.tensor_tensor(
                    out=pooledT[:, c, 0:HALO], in0=t16, in1=pj[:, HALO:2 * HALO], op=ALU.subtract),
                    reads=[t16, pj], writes=[pooledT[:, c, 0:HALO]])
            if dbg and "pooledT" in dbg:
                dma("sp", dbg_d["pooledT"], pooledT)
            if last_stage == "A3":
                return

            dma("sp", lng, lng_d, group="ln")
            dma("sp", lnb, lnb_d, group="ln")
            for tb in range(8):
                vgj = vg[tb % 2]
                bst = st(12)
                mv = st(3)
                for ct in range(2):
                    b = nb()

                    def mmv(e, tb=tb, ct=ct, b=b):
                        ins = None
                        for kc in range(16):
                            ins = e.matmul(PS(b), lhsT=hT[:, kc, tb * 128:(tb + 1) * 128],
                                           rhs=Wv[:, ct, kc, :], start=(kc == 0), stop=(kc == 15))
                        return ins
                    S.add("pe", mmv, reads=[hT[:, :, tb * 128:(tb + 1) * 128], Wv[:, ct]], writes=[PS(b)])
                    dstv = vgj[:, ct * 512:(ct + 1) * 512]
                    S.add("act", lambda e, b=b, d=dstv: e.activation(out=d, in_=PS(b), func=AF.Gelu),
                          reads=[PS(b)], writes=[dstv])
                    S.add("dve", lambda e, d=dstv, o=bst[:, ct * 6:(ct + 1) * 6]: e.bn_stats(out=o, in_=d),
                          reads=[dstv], writes=[bst[:, ct * 6:(ct + 1) * 6]])
                S.add("dve", lambda e, bst=bst, mv=mv: e.bn_aggr(out=mv[:, 0:2], in_=bst),
                      reads=[bst], writes=[mv])
                S.add("dve", lambda e, mv=mv: e.tensor_scalar(out=mv[:, 2:3], in0=mv[:, 1:2], scalar1=EPS,
                                                              scalar2=None, op0=ALU.add),
                      reads=[mv], writes=[mv])
                S.add("act", lambda e, mv=mv: e.activation(out=mv[:, 2:3], in_=mv[:, 2:3], func=AF.Sqrt),
                      reads=[mv], writes=[mv])
                S.add("dve", lambda e, mv=mv: e.reciprocal(out=mv[:, 2:3], in_=mv[:, 2:3]),
                      reads=[mv], writes=[mv])
                S.add("dve", lambda e, v=vgj, mv=mv: e.tensor_scalar(out=v, in0=v, scalar1=mv[:, 0:1],
                                                                     scalar2=mv[:, 2:3], op0=ALU.subtract,
                                                                     op1=ALU.mult),
                      reads=[vgj, mv], writes=[vgj])
                S.add("dve", lambda e, v=vgj: e.tensor_tensor(out=v, in0=v, in1=lng, op=ALU.mult),
                      reads=[vgj, lng], writes=[vgj])
                S.add("dve", lambda e, v=vgj, tb=tb: e.tensor_tensor(out=v_ln[:, tb, :], in0=v, in1=lnb,
                                                                    op=ALU.add),
                      reads=[vgj, lnb], writes=[v_ln[:, tb, :]])
            if dbg and "v_ln" in dbg:
                dma("sp", dbg_d["v_ln"], v_ln)
            if last_stage == "A1":
                return

            for m in range(8):
                g = m // 2
                for hf in range(2):
                    b = nb()

                    def mmb(e, m=m, g=g, b=b, hf=hf):
                        ins = None
                        for kc in range(2):
                            ins = e.matmul(PS(b), lhsT=wpool[:, g, kc, (m % 2) * 128:(m % 2 + 1) * 128],
                                           rhs=pooledT[:, 2 * g + kc, hf * HT:(hf + 1) * HT],
                                           start=(kc == 0), stop=(kc == 1))
                        return ins
                    S.add("pe", mmb, reads=[wpool, pooledT[:, 2 * g:2 * g + 2, hf * HT:(hf + 1) * HT]],
                          writes=[PS(b)])
                    dsty = ybT[:, m, hf * HT:(hf + 1) * HT]
                    S.add("act", lambda e, m=m, b=b, d=dsty: e.activation(out=d, in_=PS(b), func=AF.Copy,
                                                                        scale=pscT[:, m:m + 1]),
                          reads=[PS(b), pscT], writes=[dsty])
            if dbg and "ybT" in dbg:
                dma("sp", dbg_d["ybT"], ybT)
            if last_stage == "A4":
                return

            for h in range(8):
                slot = wtile(w_u[h])
                for hf in range(2):
                    b1 = nb()

                    def mmu(e, slot=slot, b1=b1, hf=hf):
                        ins = None
                        for kc in range(16):
                            ins = e.matmul(PS(b1), lhsT=slot[:, kc, :], rhs=hT[:, kc, hf * HT:(hf + 1) * HT],
                                           start=(kc == 0), stop=(kc == 15))
                        return ins
                    S.add("pe", mmu, reads=[slot, hT[:, :, hf * HT:(hf + 1) * HT]], writes=[PS(b1)])
                    ugj = ug[hf]
                    S.add("act", lambda e, b1=b1, u=ugj: e.activation(out=u, in_=PS(b1), func=AF.Gelu),
                          reads=[PS(b1)], writes=[ugj])
                    b2 = nb()

                    def mmx(e, h=h, b2=b2, hf=hf):
                        ins = None
                        for t in range(4):
                            tb = hf * 4 + t
                            ins = e.matmul(PS(b2)[:, t * 128:(t + 1) * 128],
                                           lhsT=v_ln[:, tb, h * 128:(h + 1) * 128], rhs=sgwT[:, h, :],
                                           start=True, stop=True)
                        return ins
                    S.add("pe", mmx, reads=[v_ln[:, hf * 4:(hf + 1) * 4, h * 128:(h + 1) * 128], sgwT],
                          writes=[PS(b2)])
                    tj = tmpA[hf]
                    S.add("dve", lambda e, b2=b2, tj=tj, h=h: e.tensor_tensor(
                        out=tj.rearrange("p (t i) -> p t i", t=4),
                        in0=PS(b2).rearrange("p (t i) -> p t i", t=4),
                        in1=sgub[:, h:h + 1, :].to_broadcast([128, 4, 128]), op=ALU.add),
                        reads=[PS(b2), sgub], writes=[tj])
                    dsta = yAT[:, h, hf * HT:(hf + 1) * HT]
                    S.add("dve", lambda e, tj=tj, u=ugj, d=dsta: e.tensor_tensor(out=d, in0=tj, in1=u,
                                                                              op=ALU.mult),
                          reads=[tj, ugj], writes=[dsta])
            if dbg and "yAT" in dbg:
                dma("sp", dbg_d["yAT"], yAT)
            if last_stage == "A2":
                return

            wregion()
            ringB2 = A.alloc(48 * 128 * 2, BF16, "p (k n) -> p k n", k=48)
            A.ptr = mofo_scratch
            ringB = [A.alloc(48 * 128 * 2, BF16, "p (k n) -> p k n", k=48) for _ in range(2)] + [ringB2]
            sa = [A.alloc(HT * 4, F32), A.alloc(HT * 4, F32)]
            sb = [A.alloc(HT * 4, F32), A.alloc(HT * 4, F32)]
            assert A.ptr <= mofo_scratch + 4 * D * 4
            for d in range(16):
                slot = ringB[d % 3]
                dma("pool", slot, w_gab[d])
                for hf in range(2):
                    tk = slice(hf * HT, (hf + 1) * HT)
                    bs = [hf * 4 + i for i in range(4)]

                    def mmg(e, slot=slot, bs=bs, tk=tk):
                        ins = None
                        for kc in range(16):
                            ins = e.matmul(PS(bs[0]), lhsT=slot[:, kc, :], rhs=hT[:, kc, tk],
                                           start=(kc == 0), stop=(kc == 15))
                        for kc in range(16):
                            ins = e.matmul(PS(bs[1]), lhsT=slot[:, 16 + kc, :], rhs=hT[:, kc, tk],
                                           start=(kc == 0), stop=(kc == 15))
                        for kc in range(8):
                            ins = e.matmul(PS(bs[2]), lhsT=slot[:, 32 + kc, :], rhs=yAT[:, kc, tk],
                                           start=(kc == 0), stop=(kc == 7))
                        for kc in range(8):
                            ins = e.matmul(PS(bs[3]), lhsT=slot[:, 40 + kc, :], rhs=ybT[:, kc, tk],
                                           start=(kc == 0), stop=(kc == 7))
                        return ins
                    S.add("pe", mmg, reads=[slot, hT[:, :, tk], yAT[:, :, tk], ybT[:, :, tk]],
                          writes=[PS(b) for b in bs])
                    saj, sbj = sa[hf], sb[hf]
                    S.add("act", lambda e, b=bs[0], o=saj: e.activation(out=o, in_=PS(b), func=AF.Sigmoid),
                          reads=[PS(bs[0])], writes=[saj])
                    S.add("act", lambda e, b=bs[1], o=sbj: e.activation(out=o, in_=PS(b), func=AF.Sigmoid),
                          reads=[PS(bs[1])], writes=[sbj])
                    S.add("dve", lambda e, b=bs[2], o=saj: e.tensor_tensor(out=o, in0=o, in1=PS(b), op=ALU.mult),
                          reads=[saj, PS(bs[2])], writes=[saj])
                    S.add("dve", lambda e, b=bs[3], o=sbj: e.tensor_tensor(out=o, in0=o, in1=PS(b), op=ALU.mult),
                          reads=[sbj, PS(bs[3])], writes=[sbj])
                    dstm = mergedT[:, d, tk]
                    S.add("dve", lambda e, a=saj, bb=sbj, dd=dstm: e.tensor_tensor(out=dd, in0=a, in1=bb,
                                                                                 op=ALU.add),
                          reads=[saj, sbj], writes=[dstm])
            if dbg and "mergedT" in dbg:
                dma("sp", dbg_d["mergedT"], mergedT)

        def stage_CDE(hf):
            t0 = hf * HT
            tk0 = hf * HT
            wregion()
            NRD = 3
            ringD = [A.alloc(32 * 128 * 2, BF16, "p (k n) -> p k n", k=32) for _ in range(NRD)]
            sg = [A.alloc(HT * 4, F32), A.alloc(HT * 4, F32)]
            gp_off = A.ptr
            A.ptr = hid_base
            ringC_a = A.alloc(16 * 512 * 2, BF16, "p (k n) -> p k n", k=16)
            A.ptr = pooled_base
            ringC_b = A.alloc(16 * 512 * 2, BF16, "p (k n) -> p k n", k=16)
            ringC = [ringC_a, ringC_b]
            xnb = [A.alloc(D * 2, BF16) for _ in range(3)]
            assert A.ptr <= r1_end, (A.ptr, r1_end)
            A.ptr = wbase + 12 * 1024
            ringC0 = A.alloc(16 * 512 * 2, BF16, "p (k n) -> p k n", k=16)
            A.ptr = gp_off
            gp = A.alloc(D * 4, F32)
            c_parts = [st(4) for _ in range(4)]
            ringC3 = h2T.rearrange("p k n -> p (k n)").rearrange("p (k n) -> p k n", k=16)
            slots = [ringC0 if hf == 0 else ringC[0], ringC[1] if hf == 0 else mergedT[:, :, 0:HT], ringC[0], ringC3]
            dma("pool", slots[0], w_mix[0])
            for q_ in range(3):
                dma("pool", gp[:, q_ * 512:(q_ + 1) * 512], gp1_d[:, q_ * 512:(q_ + 1) * 512])
            op_et1 = dma("pool", slots[1], w_mix[1])
            for tb in range(4):
                dma("sp", xs[:, tb, :], x_ext[HALO + t0 + tb * 128: HALO + t0 + (tb + 1) * 128, :],
                    after=[op_et1])
            dma("pool", gp[:, 3 * 512:4 * 512], gp1_d[:, 3 * 512:4 * 512])
            dma("pool", slots[3], w_mix[3])

            def mm_ev(et, tb, defer=None):
                slot = slots[et]
                b = nb()

                def mmc(e):
                    ins = None
                    for kc in range(16):
                        ins = e.matmul(PS(b), lhsT=mergedT[:, kc, tk0 + tb * 128: tk0 + (tb + 1) * 128],
                                       rhs=slot[:, kc, :], start=(kc == 0), stop=(kc == 15))
                    return ins
                S.add("pe", mmc, reads=[slot, mergedT[:, :, tk0 + tb * 128: tk0 + (tb + 1) * 128]],
                      writes=[PS(b)])
                dst = mo[:, tb, et * 512:(et + 1) * 512]

                def ev():
                    junk = xnb[(et * 4 + tb) % 3][:, 0:512]
                    part = c_parts[tb][:, et:et + 1]
                    S.add("act", lambda e: e.activation(out=junk, in_=PS(b), func=AF.Square, accum_out=part),
                          reads=[PS(b)], writes=[junk, part])
                    g_ = gp[:, et * 512:(et + 1) * 512]
                    S.add("dve", lambda e: e.tensor_tensor(out=dst, in0=PS(b), in1=g_, op=ALU.mult),
                          reads=[PS(b), g_], writes=[dst])
                if defer is not None:
                    defer.append(ev)
                else:
                    ev()

            evacs = []
            for tb in range(4):
                mm_ev(0, tb, defer=evacs)
            for ev_ in evacs:
                ev_()
            dma("pool", slots[2], w_mix[2])
            for tb in range(4):
                mm_ev(1, tb)
            mm_ev(2, 0)
            mm_ev(2, 1)
            c_mm = {0: lambda: mm_ev(3, 0), 1: lambda: mm_ev(3, 1),
                    2: lambda: (mm_ev(2, 2), mm_ev(3, 2)), 3: lambda: (mm_ev(2, 3), mm_ev(3, 3))}
            items = [dict(tb=tb, xb=xnb[tb % 3], ssq=st(1), rs=st(1), ssq2=st(1), rs2=st(1)) for tb in range(4)]

            def c_sq(it, parts=c_parts):
                p_ = parts[it["tb"]]
                S.add("dve", lambda e: e.tensor_reduce(out=it["ssq"], in_=p_, axis=mybir.AxisListType.X, op=ALU.add),
                      reads=[p_], writes=[it["ssq"]])

            def c_rstd(it):
                rstd_ops(it["ssq"], it["rs"], 128)

            def c_res(it, src=mo):
                tb = it["tb"]
                s_ = src[:, tb, :]
                x_ = xs[:, tb, :]
                S.add("dve", lambda e: e.scalar_tensor_tensor(out=x_, in0=s_, scalar=it["rs"], in1=x_,
                                                              op0=ALU.mult, op1=ALU.add),
                      reads=[s_, it["rs"], x_], writes=[x_])

            def c_sq2(it):
                s_ = xs[:, it["tb"], :]
                S.add("act", lambda e: e.activation(out=it["xb"], in_=s_, func=AF.Square, accum_out=it["ssq2"]),
                      reads=[s_], writes=[it["xb"], it["ssq2"]])

            def c_rstd2(it):
                rstd_ops(it["ssq2"], it["rs2"], 128)

            def c_scale(it):
                s_ = xs[:, it["tb"], :]
                S.add("dve", lambda e: e.tensor_scalar(out=it["xb"], in0=s_, scalar1=it["rs2"], scalar2=None,
                                                       op0=ALU.mult),
                      reads=[s_, it["rs2"]], writes=[it["xb"]])

            def c_tr(it):
                transpose_evac(it["xb"], 128, g2T, h2T, it["tb"] * 128)

            def d_sub(f, lo, hi):
                slot = ringD[f % NRD]
                b1, b2 = nb(), nb()

                def mmd(e):
                    ins = None
                    for kc in range(16):
                        ins = e.matmul(PS(b1)[:, lo:hi], lhsT=slot[:, kc, :], rhs=h2T[:, kc, lo:hi],
                                       start=(kc == 0), stop=(kc == 15))
                    for kc in range(16):
                        ins = e.matmul(PS(b2)[:, lo:hi], lhsT=slot[:, 16 + kc, :], rhs=h2T[:, kc, lo:hi],
                                       start=(kc == 0), stop=(kc == 15))
                    return ins
                S.add("pe", mmd, reads=[slot, h2T[:, :, lo:hi]], writes=[PS(b1), PS(b2)])
                sgj = sg[f % 2]
                S.add("act", lambda e: e.activation(out=sgj[:, lo:hi], in_=PS(b1)[:, lo:hi], func=AF.Silu),
                      reads=[PS(b1)], writes=[sgj[:, lo:hi]])
                S.add("dve", lambda e: e.tensor_tensor(out=hidT[:, f, lo:hi], in0=sgj[:, lo:hi],
                                                       in1=PS(b2)[:, lo:hi], op=ALU.mult),
                      reads=[sgj[:, lo:hi], PS(b2)], writes=[hidT[:, f, lo:hi]])

            NSUB = NRD if last_stage not in ("C",) else 0
            for f in range(NSUB):
                dma("pool", ringD[f % NRD], w_gu[f])

            def c_dA(it):
                if it["tb"] == 1:
                    for f in range(NSUB):
                        d_sub(f, 0, 256)

            wavefront([lambda it: c_mm[it["tb"]](), c_sq, c_rstd, c_res, c_sq2, c_rstd2, c_scale, c_tr, c_dA],
                      items)
            for f in range(NSUB):
                d_sub(f, 256, 512)
            if dbg and "x1" in dbg and hf == dbg.get("_hf", 0):
                dma("sp", dbg_d["x1"].rearrange("(t p) d -> p t d", p=128), xs)
                dma("sp", dbg_d["h2T"], h2T)
            if last_stage == "C":
                return

            for f in range(NSUB, NFC):
                slot = ringD[f % NRD]
                dma("pool", slot, w_gu[f])
                b1, b2 = nb(), nb()

                def mmd(e, slot=slot, b1=b1, b2=b2):
                    ins = None
                    for kc in range(16):
                        ins = e.matmul(PS(b1), lhsT=slot[:, kc, :], rhs=h2T[:, kc, :],
                                       start=(kc == 0), stop=(kc == 15))
                    for kc in range(16):
                        ins = e.matmul(PS(b2), lhsT=slot[:, 16 + kc, :], rhs=h2T[:, kc, :],
                                       start=(kc == 0), stop=(kc == 15))
                    return ins
                S.add("pe", mmd, reads=[slot, h2T], writes=[PS(b1), PS(b2)])
                sgj = sg[f % 2]
                S.add("act", lambda e, b1=b1, o=sgj: e.activation(out=o, in_=PS(b1), func=AF.Silu),
                      reads=[PS(b1)], writes=[sgj])
                S.add("dve", lambda e, b2=b2, o=sgj, f=f: e.tensor_tensor(out=hidT[:, f, :], in0=o, in1=PS(b2),
                                                                         op=ALU.mult),
                      reads=[sgj, PS(b2)], writes=[hidT[:, f, :]])
            if last_stage == "D":
                return

            wregion()
            NRE = 7
            ringE = [A.alloc(4 * 512 * 2, BF16, "p (k n) -> p k n", k=4) for _ in range(NRE)]
            assert A.ptr <= gp_off
            A.ptr = gp_off
            gp2 = A.alloc(D * 4, F32)
            xne = [A.alloc(512 * 2, BF16) for _ in range(2)]
            assert A.ptr <= wbase + wsize
            dma("pool", gp2, gp2_d)
            e_parts = [st(4) for _ in range(4)]
            re_ = 0
            for et in range(4):
                banks = [(et % 2) * 4 + tb for tb in range(4)]
                for fg in range(11):
                    slot = ringE[re_ % NRE]
                    re_ += 1
                    dma("pool", slot, w_dn[et, fg])

                    def mme(e, slot=slot, fg=fg, banks=banks):
                        ins = None
                        for fl in range(4):
                            fc = fg * 4 + fl
                            for tb in range(4):
                                ins = e.matmul(PS(banks[tb]), lhsT=hidT[:, fc, tb * 128:(tb + 1) * 128],
                                               rhs=slot[:, fl, :], start=(fc == 0), stop=(fc == NFC - 1))
                        return ins
                    S.add("pe", mme, reads=[slot, hidT[:, fg * 4:(fg + 1) * 4, :]], writes=[PS(b) for b in banks])
                for tb in range(4):
                    b = banks[tb]
                    dst = fo[:, tb, et * 512:(et + 1) * 512]
                    junk = xne[tb % 2]
                    part = e_parts[tb][:, et:et + 1]
                    S.add("act", lambda e, b=b, j=junk, p=part: e.activation(out=j, in_=PS(b), func=AF.Square,
                                                                             accum_out=p),
                          reads=[PS(b)], writes=[junk, part])
                    g_ = gp2[:, et * 512:(et + 1) * 512]
                    S.add("dve", lambda e, b=b, d=dst, g_=g_: e.tensor_tensor(out=d, in0=PS(b), in1=g_, op=ALU.mult),
                          reads=[PS(b), g_], writes=[dst])
            items = [dict(tb=tb, xb=xne[tb % 2], ssq=st(1), rs=st(1)) for tb in range(4)]

            def e_out(it):
                tb = it["tb"]
                o_ = out_d[t0 + tb * 128: t0 + (tb + 1) * 128, :]
                S.add("sp", lambda e: e.dma_start(out=o_, in_=xs[:, tb, :]),
                      reads=[xs[:, tb, :]], writes=[("dram", "out", tb)], dma=True)

            wavefront([lambda it: c_sq(it, parts=e_parts), c_rstd], items)
            wavefront([lambda it: c_res(it, src=fo), e_out], items)

        stage_AB()
        if last_stage in (None, "C", "D", "E"):
            for hf in range(nhalves):
                stat_ctr[0] = 128
                stage_CDE(hf)

        fin = S.add("sp", None, reads=[out_d] + list(dbg_d.values()), writes=[])

        S.analyze()
        outnames = {("dram", "out")} | {("dram", "dbg_" + n) for n in dbg_d}
        for i, op in enumerate(S.ops):
            if op.dma and any((w in outnames or w[:2] == ("dram", "out")) for w in op.W) and i not in fin.deps:
                fin.deps.append(i)

        def mksem(name):
            h = es.enter_context(nc.semaphore("s" + name))
            sems.append(h)
            return h

        run = S.emit(mksem)
        with nc.Block() as block:
            @block.sync
            def _(e):
                run("sp", e)

            @block.gpsimd
            def _(e):
                run("pool", e)

            @block.tensor
            def _(e):
                run("pe", e)

            @block.scalar
            def _(e):
                run("act", e)

            @block.vector
            def _(e):
                run("dve", e)
    return nc, S


def _tile_cols(W, ncol):
    K, N = W.shape
    return np.ascontiguousarray(W.reshape(K // 128, 128, N // ncol, ncol).transpose(2, 1, 0, 3))


def prepare_inputs(x, norm1_pre, w_in, v_ln_g, v_ln_b, sgu_w, sgu_b, pool_w, pool_scale,
                   w_a_out, w_b_out, w_mix_out, norm1_post, norm2_pre, w_ffn_gate,
                   w_ffn_up, w_ffn_down, norm2_post):
    f32 = np.float32
    x = np.asarray(x, f32)
    W_in = np.asarray(w_in, f32)[0]
    shared = {}
    shared["w_u"] = _tile_cols(W_in[:, 0:1024], 128)
    shared["w_v"] = _tile_cols(W_in[:, 1024:2048], 512)
    shared["w_p"] = _tile_cols(W_in[:, 2048:3072], 128)
    ga = _tile_cols(W_in[:, 3072:5120], 128)
    gb = _tile_cols(W_in[:, 5120:7168], 128)
    wa = _tile_cols(np.asarray(w_a_out, f32)[0], 128)
    wb = _tile_cols(np.asarray(w_b_out, f32)[0], 128)
    shared["w_gab"] = np.ascontiguousarray(np.concatenate([ga, gb, wa, wb], axis=2))
    pw = np.asarray(pool_w, f32)[0]
    shared["w_pool"] = np.ascontiguousarray(pw.reshape(4, 2, 128, 256).transpose(2, 0, 1, 3))
    shared["w_mix"] = _tile_cols(np.asarray(w_mix_out, f32)[0], 512)
    wg = _tile_cols(np.asarray(w_ffn_gate, f32)[0], 128)
    wu = _tile_cols(np.asarray(w_ffn_up, f32)[0], 128)
    shared["w_gu"] = np.ascontiguousarray(np.concatenate([wg, wu], axis=2))
    wd = np.asarray(w_ffn_down, f32)[0]
    shared["w_dn"] = np.ascontiguousarray(wd.reshape(11, 4, 128, 4, 512).transpose(3, 0, 2, 1, 4))
    sw = np.asarray(sgu_w, f32)[0]
    shared["sgwT"] = np.ascontiguousarray(sw.transpose(2, 0, 1))
    shared["sgub"] = np.ascontiguousarray(np.broadcast_to(np.asarray(sgu_b, f32)[0][None], (128, 8, 128)))
    shared["lng"] = np.ascontiguousarray(np.broadcast_to(np.asarray(v_ln_g, f32)[0][None], (128, DS)))
    shared["lnb"] = np.ascontiguousarray(np.broadcast_to(np.asarray(v_ln_b, f32)[0][None], (128, DS)))
    shared["gp1"] = np.ascontiguousarray(np.broadcast_to(np.asarray(norm1_post, f32)[0][None], (128, D)))
    shared["gp2"] = np.ascontiguousarray(np.broadcast_to(np.asarray(norm2_post, f32)[0][None], (128, D)))
    shared["g1T"] = np.ascontiguousarray(np.asarray(norm1_pre, f32)[0].reshape(16, 128).T)
    shared["g2T"] = np.ascontiguousarray(np.asarray(norm2_pre, f32)[0].reshape(16, 128).T)
    shared["pscT"] = np.ascontiguousarray(np.asarray(pool_scale, f32)[0].reshape(8, 128).T)
    shared["ident"] = np.eye(128, dtype=f32).astype(ml_dtypes.bfloat16)
    inv_std = np.zeros((4, 16), f32)
    inv_first = np.zeros((4, 16), f32)
    for g in range(4):
        w = 2 << g
        inv_std[g, :] = 1.0 / w
        inv_first[g, :] = 1.0 / np.minimum(np.arange(1, 17), w)
    in_maps = []
    S = x.shape[1]
    cores_per_b = S // TOK
    for c in range(NCORES):
        b, q = divmod(c, cores_per_b)
        t0 = q * TOK
        xe = np.zeros((HALO + TOK, D), f32)
        xe[HALO:] = x[b, t0:t0 + TOK]
        if q > 0:
            xe[:HALO] = x[b, t0 - HALO:t0]
        m = dict(shared)
        m["x_ext"] = xe
        iv = inv_first if q == 0 else inv_std
        m["invf"] = np.ascontiguousarray(np.broadcast_to(iv[None], (128, 4, 16)))
        in_maps.append(m)
    return in_maps


_CACHE = {}


def kernel(**inputs):
    in_maps = prepare_inputs(**inputs)
    if "nc" not in _CACHE:
        _CACHE["nc"] = build_program()[0]
    nc = _CACHE["nc"]
    res = run_bass_kernel_spmd(nc, in_maps, core_ids=list(range(NCORES)))
    x = inputs["x"]
    B, S, _ = x.shape
    out = np.empty((B, S, D), np.float32)
    cores_per_b = S // TOK
    for c in range(NCORES):
        b, q = divmod(c, cores_per_b)
        out[b, q * TOK:(q + 1) * TOK] = res.results[c]["out"]
    return out
```
